# Optimizing a Trainium2 kernel written in Bass

```python
import jax, jax.numpy as jnp
from jax import lax
import numpy as np

D_MODEL = 2048
BATCH = 4
SEQ = 2048
DEPTH = 1
DEC_BATCH = 128
DEC_SEQ = 4
PAST_LEN = 16384
PAGE_SIZE = 128

GLA_HEADS = 4
GLA_DK = D_MODEL // 2 // GLA_HEADS
GLA_DV = D_MODEL // GLA_HEADS
GLA_RANK = 16
GLA_TAU = 16.0
GLA_CHUNK = 16
RET_HEADS = 8
RET_DK = D_MODEL // 2 // RET_HEADS
RET_DV = D_MODEL // RET_HEADS
RET_CHUNK = 64
ROPE_BASE = 10000.0
LN_EPS = 1e-5
HEAD_NORM_EPS = 1e-6
DEEPNORM_ALPHA = (2.0 * DEPTH) ** 0.25
DEEPNORM_BETA = (8.0 * DEPTH) ** -0.25

QK_A = GLA_HEADS * GLA_DK
V_A = GLA_HEADS * GLA_DV
QK_B = RET_HEADS * RET_DK
V_B = RET_HEADS * RET_DV
SPLITS = (QK_A, QK_A, V_A, V_A, GLA_RANK, QK_B, QK_B, V_B, V_B, D_MODEL, D_MODEL)
D_IN = sum(SPLITS)

kernel_name = "gla_retnet_parallel_gated_deepnorm_step"


def _split_cols(h):
    idx, acc = [], 0
    for s in SPLITS[:-1]:
        acc += s
        idx.append(acc)
    return jnp.split(h, idx, axis=-1)


def _pad_time(a, n_pad):
    if n_pad == 0:
        return a
    return jnp.pad(a, [(0, 0), (0, n_pad)] + [(0, 0)] * (a.ndim - 2))


def _to_chunks(t, n, c):
    B = t.shape[0]
    return t.reshape(B, n, c, *t.shape[2:]).transpose(1, 0, 3, 2, *range(4, t.ndim + 1))


def gla_chunked(q, k, v, log_a, s0):
    B, L, H, _ = q.shape
    c = min(GLA_CHUNK, L)
    n = -(-L // c)
    pad = n * c - L
    f32 = jnp.float32
    qc, kc, vc, ac = [_to_chunks(_pad_time(t.astype(f32), pad), n, c) for t in (q, k, v, log_a)]
    mask = jnp.tril(jnp.ones((c, c), dtype=bool))

    def step(S, inp):
        qi, ki, vi, ai = inp
        b = jnp.cumsum(ai, axis=2)
        b_last = b[:, :, -1, :]
        o_inter = jnp.einsum('bhtk,bhkv->bhtv', qi * jnp.exp(b), S)
        rel = b[:, :, :, None, :] - b[:, :, None, :, :]
        rel = jnp.exp(jnp.where(mask[:, :, None], rel, -jnp.inf))
        scores = jnp.einsum('bhtk,bhsk,bhtsk->bhts', qi, ki, rel)
        o_intra = jnp.einsum('bhts,bhsv->bhtv', scores, vi)
        S_new = jnp.exp(b_last)[..., None] * S + jnp.einsum(
            'bhsk,bhsv->bhkv', ki * jnp.exp(b_last[:, :, None, :] - b), vi)
        return S_new, o_inter + o_intra

    S, o = lax.scan(step, s0.astype(f32), (qc, kc, vc, ac))
    o = o.transpose(1, 0, 3, 2, 4).reshape(B, n * c, H, -1)[:, :L]
    return o, S.astype(s0.dtype)


def retention_chunked(q, k, v, log_g, s0):
    B, L, H, _ = q.shape
    c = min(RET_CHUNK, L)
    n = -(-L // c)
    pad = n * c - L
    f32 = jnp.float32
    qc, kc, vc = [_to_chunks(_pad_time(t.astype(f32), pad), n, c) for t in (q, k, v)]
    gc = _pad_time(log_g.astype(f32), pad).reshape(B, n, c, H).transpose(1, 0, 3, 2)
    mask = jnp.tril(jnp.ones((c, c), dtype=bool))

    def step(S, inp):
        qi, ki, vi, gi = inp
        b = jnp.cumsum(gi, axis=-1)
        b_last = b[:, :, -1]
        o_inter = jnp.einsum('bhtk,bhkv->bhtv', qi * jnp.exp(b)[..., None], S)
        decay = jnp.exp(jnp.where(mask, b[:, :, :, None] - b[:, :, None, :], -jnp.inf))
        scores = jnp.einsum('bhtk,bhsk->bhts', qi, ki) * decay
        o_intra = jnp.einsum('bhts,bhsv->bhtv', scores, vi)
        S_new = jnp.exp(b_last)[..., None, None] * S + jnp.einsum(
            'bhsk,bhsv->bhkv', ki * jnp.exp(b_last[..., None] - b)[..., None], vi)
        return S_new, o_inter + o_intra

    S, o = lax.scan(step, s0.astype(f32), (qc, kc, vc, gc))
    o = o.transpose(1, 0, 3, 2, 4).reshape(B, n * c, H, -1)[:, :L]
    return o, S.astype(s0.dtype)


def _rotate(x, pos):
    f32 = jnp.float32
    inv = 1.0 / (ROPE_BASE ** jnp.linspace(0.0, 1.0, RET_DK // 2, dtype=f32))
    ang = pos.astype(f32)[:, None] * inv[None, :]
    sin = jnp.sin(ang)[None, :, None, :]
    cos = jnp.cos(ang)[None, :, None, :]
    xf = x.astype(f32)
    x1, x2 = xf[..., 0::2], xf[..., 1::2]
    return jnp.stack([x1 * cos - x2 * sin, x2 * cos + x1 * sin], axis=-1).reshape(x.shape)


def _ret_log_decay():
    return jnp.log(1.0 - 2.0 ** (-5.0 - jnp.arange(RET_HEADS, dtype=jnp.float32)))


def _rms_heads(o, w):
    o = o * lax.rsqrt(jnp.mean(o * o, axis=-1, keepdims=True) + HEAD_NORM_EPS) * w.astype(jnp.float32)
    return o.reshape(o.shape[0], o.shape[1], -1)


def _group_norm(o, w, b):
    mu = jnp.mean(o, axis=-1, keepdims=True)
    d = o - mu
    var = jnp.mean(d * d, axis=-1, keepdims=True)
    o = d * lax.rsqrt(var + HEAD_NORM_EPS)
    o = o * w.astype(jnp.float32).reshape(RET_HEADS, RET_DV) + b.astype(jnp.float32).reshape(RET_HEADS, RET_DV)
    return o.reshape(o.shape[0], o.shape[1], -1)


def _layer_norm(x, w, b):
    xf = x.astype(jnp.float32)
    mu = jnp.mean(xf, axis=-1, keepdims=True)
    d = xf - mu
    var = jnp.mean(d * d, axis=-1, keepdims=True)
    return d * lax.rsqrt(var + LN_EPS) * w.astype(jnp.float32) + b.astype(jnp.float32)


def hybrid_layer(x, pos, s_gla0, s_ret0, w_in, w_lr, b_lr, gla_norm_w, ret_norm_w, ret_norm_b, w_out, ln_w, ln_b):
    B, L, _ = x.shape
    f32 = jnp.float32
    h = jnp.einsum('bld,de->ble', x, w_in)
    qa, ka, va, za, lra, qb, kb, vb, zb, ga, gb = _split_cols(h)

    qa = qa.reshape(B, L, GLA_HEADS, GLA_DK) * (GLA_DK ** -0.5)
    ka = ka.reshape(B, L, GLA_HEADS, GLA_DK)
    va = va.reshape(B, L, GLA_HEADS, GLA_DV)
    gate_logit = (jnp.einsum('blr,rk->blk', lra, w_lr) + b_lr).astype(f32)
    log_a = (jax.nn.log_sigmoid(gate_logit) / GLA_TAU).reshape(B, L, GLA_HEADS, GLA_DK)
    o_a, s_gla = gla_chunked(qa, ka, va, log_a, s_gla0)
    o_a = _rms_heads(o_a, gla_norm_w) * jax.nn.silu(za.astype(f32))

    qb = _rotate(qb.reshape(B, L, RET_HEADS, RET_DK), pos)
    kb = _rotate(kb.reshape(B, L, RET_HEADS, RET_DK), pos) * (RET_DK ** -0.5)
    vb = vb.reshape(B, L, RET_HEADS, RET_DV)
    log_g = jnp.broadcast_to(_ret_log_decay(), (B, L, RET_HEADS))
    o_b, s_ret = retention_chunked(qb, kb, vb, log_g, s_ret0)
    o_b = _group_norm(o_b, ret_norm_w, ret_norm_b) * jax.nn.silu(zb.astype(f32))

    merged = jax.nn.sigmoid(ga.astype(f32)) * o_a + jax.nn.sigmoid(gb.astype(f32)) * o_b
    y = jnp.einsum('ble,ed->bld', merged.astype(x.dtype), w_out)
    x_new = _layer_norm(DEEPNORM_ALPHA * x + y, ln_w, ln_b)
    return x_new.astype(x.dtype), s_gla, s_ret


def setup_inputs(seed: int = 0) -> dict:
    key = jax.random.key(seed)
    ks = jax.random.split(key, 20)
    f32 = jnp.float32
    x_prompt = jax.random.normal(ks[0], (BATCH, SEQ, D_MODEL), f32)
    x_sample = jax.random.normal(ks[1], (DEC_BATCH, DEC_SEQ, D_MODEL), f32)
    state_gla = 0.5 * jax.random.normal(ks[2], (DEPTH, DEC_BATCH, GLA_HEADS, GLA_DK, GLA_DV), f32)
    state_ret = 0.5 * jax.random.normal(ks[3], (DEPTH, DEC_BATCH, RET_HEADS, RET_DK, RET_DV), f32)
    col_scale = []
    for i, s in enumerate(SPLITS):
        scale = DEEPNORM_BETA if i in (2, 7) else 1.0
        col_scale.append(jnp.full((s,), scale, f32))
    col_scale = jnp.concatenate(col_scale)
    w_in = jax.random.normal(ks[4], (DEPTH, D_MODEL, D_IN), f32) * (D_MODEL ** -0.5) * col_scale
    w_lr = jax.random.normal(ks[5], (DEPTH, GLA_RANK, QK_A), f32) * (GLA_RANK ** -0.5)
    b_lr = 1.0 + 0.5 * jax.random.normal(ks[6], (DEPTH, QK_A), f32)
    gla_norm_w = 1.0 + 0.02 * jax.random.normal(ks[7], (DEPTH, GLA_DV), f32)
    ret_norm_w = 1.0 + 0.02 * jax.random.normal(ks[8], (DEPTH, V_B), f32)
    ret_norm_b = 0.02 * jax.random.normal(ks[9], (DEPTH, V_B), f32)
    w_out = jax.random.normal(ks[10], (DEPTH, V_A, D_MODEL), f32) * (V_A ** -0.5) * DEEPNORM_BETA
    ln_w = 1.0 + 0.02 * jax.random.normal(ks[11], (DEPTH, D_MODEL), f32)
    ln_b = 0.02 * jax.random.normal(ks[12], (DEPTH, D_MODEL), f32)
    return {"x_prompt": x_prompt, "x_sample": x_sample, "state_gla": state_gla, "state_ret": state_ret,
            "w_in": w_in, "w_lr": w_lr, "b_lr": b_lr, "gla_norm_w": gla_norm_w,
            "ret_norm_w": ret_norm_w, "ret_norm_b": ret_norm_b, "w_out": w_out,
            "ln_w": ln_w, "ln_b": ln_b}


def reference(x_prompt, x_sample, state_gla, state_ret, w_in, w_lr, b_lr, gla_norm_w,
              ret_norm_w, ret_norm_b, w_out, ln_w, ln_b):
    pos_prompt = jnp.arange(x_prompt.shape[1], dtype=jnp.int32)
    pos_sample = PAST_LEN + jnp.arange(x_sample.shape[1], dtype=jnp.int32)
    hp, hs = x_prompt, x_sample
    gla_p, ret_p, gla_s, ret_s = [], [], [], []
    for l in range(DEPTH):
        params = (w_in[l], w_lr[l], b_lr[l], gla_norm_w[l], ret_norm_w[l], ret_norm_b[l],
                  w_out[l], ln_w[l], ln_b[l])
        zero_gla = jnp.zeros((hp.shape[0], GLA_HEADS, GLA_DK, GLA_DV), state_gla.dtype)
        zero_ret = jnp.zeros((hp.shape[0], RET_HEADS, RET_DK, RET_DV), state_ret.dtype)
        hp, sg_p, sr_p = hybrid_layer(hp, pos_prompt, zero_gla, zero_ret, *params)
        hs, sg_s, sr_s = hybrid_layer(hs, pos_sample, state_gla[l], state_ret[l], *params)
        gla_p.append(sg_p)
        ret_p.append(sr_p)
        gla_s.append(sg_s)
        ret_s.append(sr_s)
    new_gla_prompt = jnp.stack(gla_p)
    new_ret_prompt = jnp.stack(ret_p)
    new_gla_sample = jnp.stack(gla_s)
    new_ret_sample = jnp.stack(ret_s)
    return (hp, hs, new_gla_prompt, new_ret_prompt, new_gla_sample, new_ret_sample)
```

```python
import math
from contextlib import ExitStack

import numpy as np
import concourse.bass as bass
import concourse.mybir as mybir
from concourse.bass_utils import run_bass_kernel_spmd

F32 = mybir.dt.float32
BF16 = mybir.dt.bfloat16
AF = mybir.ActivationFunctionType
ALU = mybir.AluOpType

D = 2048
NT = 17
NSEQ = 32
WC = 1040
NU = 10
NCOL = 2176
LN_EPS = 1e-5
HN_EPS = 1e-6
ALPHA = 2.0 ** 0.25
KINDS = ["arec", "agate", "ret", "ret", "arec", "agate", "ret", "ret"]


import os as _os
HOP = float(_os.environ.get('K_HOP', '400'))
CSC = float(_os.environ.get('K_CSC', '1.4'))
PEPRIO = int(_os.environ.get('K_PEPRIO', '2'))
PSPLIT = int(_os.environ.get('K_PSPLIT', '4'))
DMABW = float(_os.environ.get('K_DMABW', '150'))
SAMEWAR = int(_os.environ.get('K_SAMEWAR', '1'))


class Res:
    __slots__ = ("w", "r", "psum")

    def __init__(self, psum=False):
        self.w = None
        self.r = []
        self.psum = psum


class Op:
    __slots__ = ("stream", "fn", "kind", "key", "raw", "war", "idx", "issue", "lat", "ready_t", "finish", "tok", "nusers", "prio")

    def __init__(self, stream, fn, kind, key, raw, war, idx, issue, lat):
        self.stream, self.fn, self.kind, self.key = stream, fn, kind, key
        self.raw, self.war, self.idx, self.issue, self.lat = raw, war, idx, issue, lat
        self.ready_t = 0.0
        self.finish = 0.0
        self.tok = None
        self.nusers = 0
        self.prio = 1


class Builder:
    STREAMS = ("pe", "act", "dve", "pool", "sp")

    def __init__(self):
        self.all = []
        self.extra = []

    def _deps(self, reads, writes, extra):
        raw, war = set(extra), set()
        raw.update(self.extra)
        for r in reads:
            if r.w is not None:
                raw.add(r.w)
            if r.psum:
                war.update(r.r)
        for w in writes:
            if w.w is not None:
                raw.add(w.w)
            war.update(w.r)
        war -= raw
        return raw, war

    def _add(self, op, reads, writes):
        for d in op.raw:
            d.nusers += 1
        for d in op.war:
            d.nusers += 1
        self.all.append(op)
        for r in reads:
            r.r.append(op)
        for w in writes:
            w.w = op
            w.r = []
        return op

    def op(self, eng, fn, reads=(), writes=(), extra=(), special=None, cost=None):
        raw, war = self._deps(reads, writes, extra)
        c = cost if cost is not None else getattr(fn, "cost", 300.0)
        if eng == "pool" and cost is None:
            c = getattr(fn, "pcost", 2.0 * c)
        if eng != "pe":
            c *= CSC
        kind = "cc" if special is not None else "eng"
        o = Op(eng, fn, kind, special, raw, war, len(self.all), c, HOP if special is None else 30000.0)
        return self._add(o, reads, writes)

    def dma(self, key, fn, reads=(), writes=(), extra=(), q="sp"):
        raw, war = self._deps(reads, writes, extra)
        nbytes = getattr(fn, "nbytes", 65536)
        o = Op(q, fn, "dma", key, raw, war, len(self.all), 100.0 if q == "sp" else 700.0, 2500.0 + nbytes / DMABW)
        return self._add(o, reads, writes)

    def sinks(self):
        return [o for o in self.all if o.nusers == 0]

    def schedule(self):
        users = {}
        nd = {}
        for o in self.all:
            deps = o.raw | o.war
            nd[o] = len(deps)
            for d in deps:
                users.setdefault(d, []).append(o)
                if o.stream == "pe" and PEPRIO:
                    d.prio = 0
        if PEPRIO == 2:
            cpl = {}
            for o in reversed(self.all):
                m = 0.0
                for u in users.get(o, ()):
                    if cpl[u] > m:
                        m = cpl[u]
                cpl[o] = m + o.issue + o.lat
                o.prio = -cpl[o]
        ready = {st: [] for st in self.STREAMS}
        free = {st: 0.0 for st in self.STREAMS}
        order = {st: [] for st in self.STREAMS}
        for o in self.all:
            if nd[o] == 0:
                ready[o.stream].append(o)
        remaining = len(self.all)
        while remaining:
            best = None
            for st in self.STREAMS:
                lst = ready[st]
                if not lst:
                    continue
                f = free[st]
                cand = min(lst, key=lambda o: (max(o.ready_t, f), o.prio, o.idx))
                t = max(cand.ready_t, f)
                if best is None or (t, cand.idx) < (best[0], best[1].idx):
                    best = (t, cand)
            t, o = best
            ready[o.stream].remove(o)
            free[o.stream] = t + o.issue
            o.finish = t + o.issue + o.lat
            order[o.stream].append(o)
            for u in users.get(o, ()):
                nd[u] -= 1
                if o.finish > u.ready_t:
                    u.ready_t = o.finish
                if nd[u] == 0:
                    ready[u.stream].append(u)
            remaining -= 1
        self.makespan = max(free.values())
        return order


def build_program():
    nc = bass.Bass("TRN2", target_bir_lowering=False)
    B = Builder()

    def din(name, shape, dt=F32):
        return nc.dram_tensor(name, list(shape), dt, kind="ExternalInput").ap()

    def dout(name, shape, dt=F32):
        return nc.dram_tensor(name, list(shape), dt, kind="ExternalOutput").ap()

    xT_d = din("xT", [NT, 128, 16 * 128])
    xres_d = din("xres", [1088, D])
    wu_d = din("wu", [NU, D, WC])
    wlr_d = din("wlr", [16, 512])
    blr_d = din("blr", [1, 512])
    gw_d = din("gw", [1, 512])
    rw_d = din("rw", [1, 1024])
    rb_d = din("rb", [1, 1024])
    lnw_d = din("lnw", [1, D])
    lnb_d = din("lnb", [1, D])
    sgla_d = din("sgla", [NSEQ, 2, 256, 512])
    sret_d = din("sret", [NSEQ, 4, 128, 256])
    rope_d = din("rope", [4, NT, 128, 512])
    cst_d = din("cst", [128, 1024])
    cstb_d = din("cstb", [128, 512], BF16)

    y_d = dout("y", [1088, D])
    sgp_d = dout("sgp", [2, 256, 512])
    srp_d = dout("srp", [4, 128, 256])
    sgs_d = dout("sgs", [NSEQ, 2, 256, 512])
    srs_d = dout("srs", [NSEQ, 4, 128, 256])

    ib_d = [nc.dram_tensor("ib%d" % j, [256, NCOL], BF16).ap() for j in range(4)]
    ob_all = nc.dram_tensor("ob_all", [2048, NCOL], BF16).ap()
    ob_d = [ob_all[j * 512:(j + 1) * 512, :] for j in range(4)]
    pending_cc = []
    cc_toks = {}

    def flush_cc():
        while pending_cc:
            j, tok = pending_cc.pop(0)
            cc_toks[j] = B.op("pool", lambda e, j=j: e.collective_compute(
                "AllGather", ALU.bypass, replica_groups=[[0, 1], [2, 3], [4, 5], [6, 7]],
                ins=[ib_d[j].opt()], outs=[ob_d[j].opt()]), extra=[tok], special="cc%d" % j)

    es = ExitStack()
    arena = es.enter_context(nc.sbuf_tensor("arena", [128, 53000], F32))
    banks = [es.enter_context(nc.psum_tensor("pb%d" % i, [128, 512], F32)) for i in range(8)]
    bres = [Res(psum=True) for _ in range(8)]

    class Carver:
        def __init__(self, start=0):
            self.off = start

        def take(self, nbytes):
            nb = (nbytes + 63) // 64 * 64
            o = self.off
            self.off += nb
            assert self.off <= 53000 * 4, self.off
            return o

    def view(off, nelem, dt, parts=128):
        if dt == F32:
            return arena[0:parts, off // 4: off // 4 + nelem]
        nw = (nelem + 1) // 2
        return arena[0:parts, off // 4: off // 4 + nw].bitcast(BF16)[:, 0:nelem]

    class Tile:
        def __init__(self, car, nelem, dt, parts=128):
            self.off = car.take(nelem * (4 if dt == F32 else 2))
            self.n = nelem
            self.dt = dt
            self.ap = view(self.off, nelem, dt, parts)
            self.res = Res()

        def v3(self, a):
            return self.ap.rearrange("p (a b) -> p a b", a=a)

    car = Carver()
    cst = Tile(car, 1024, F32)
    cstb = Tile(car, 512, BF16)
    gw = Tile(car, 512, F32)
    rwt = Tile(car, 256, F32)
    rbt = Tile(car, 256, F32)
    wlr = Tile(car, 512, BF16, parts=32)
    wbuf = [Tile(car, 16 * WC, BF16) for _ in range(2)]
    wbres = [[Res() for _ in range(16)] for _ in range(2)]
    main_start = car.off

    UCP = cst.ap[:, 0:128]
    UCS = cst.ap[:, 128:256]
    ROWSEL = cst.ap[:, 256:288]
    SELNEG = cst.ap[:, 288:320]
    NEGCOL = cst.ap[:, 320:322]
    ONES = cst.ap[0:1, 384:512]
    RT = cst.ap[:, 512:544]
    C_LNQ = cst.ap[:, 544:545]
    C_HEPS = cst.ap[:, 545:546]
    C_LEPS = cst.ap[:, 546:547]
    C_ONE = cst.ap[:, 547:548]
    C_H0 = cst.ap[:, 548:549]
    C_H1 = cst.ap[:, 549:550]
    IDENT = cstb.ap[:, 0:128]
    MASKP = cstb.ap[:, 128:256]
    MASKS = cstb.ap[:, 256:384]

    car.off = main_start
    xst = [Tile(car, 2048, F32) for _ in range(1)]
    xtb = [Tile(car, 2048, BF16) for _ in range(3)]
    partA = Tile(car, NT * 512, F32)
    partA_res = [Res() for _ in range(NT)]
    mTb = Tile(car, 2 * NCOL, BF16)
    S = Tile(car, 1024, F32)
    Sbf = Tile(car, 1024, BF16)
    S0 = [Tile(car, 1024, F32) for _ in range(2)]
    S0bf = [Tile(car, 1024, BF16) for _ in range(2)]
    Sout = [Tile(car, 1024, F32) for _ in range(2)]
    qTs = Tile(car, 256, BF16)
    kts = Tile(car, 256, BF16)
    vs = Tile(car, 512, BF16)
    esm = Tile(car, 64, F32)
    oacc = Tile(car, 512, F32)
    qTm = [Tile(car, 256, BF16) for _ in range(2)]
    gs = Tile(car, 256, F32)
    ktm = [Tile(car, 256, BF16) for _ in range(2)]

    class WorkSet:
        pass

    wsets = []
    for _ in range(2):
        w = WorkSet()
        base = car.off
        w.lraT = Tile(car, 128, BF16, parts=32)
        w.esp = Tile(car, 512, F32)
        w.EE = Tile(car, 512, F32)

        class _A:
            pass
        def alias(t, a, b2):
            o = _A()
            o.ap = t.ap[:, a:b2]
            o.res = t.res
            return o
        w.ex = alias(w.esp, 0, 256)
        w.sp = alias(w.esp, 256, 512)
        w.Eb = alias(w.EE, 0, 256)
        w.Enb = alias(w.EE, 256, 512)
        w.sz = alias(w.esp, 0, 512)
        w.sg = alias(w.EE, 0, 512)
        w.on = alias(w.esp, 256, 512)
        w.rope = alias(w.EE, 256, 512)
        w.ecol = Tile(car, 16, F32)
        w.qk = Tile(car, 512, BF16)
        w.qtl = alias(w.qk, 0, 256)
        w.ktl = alias(w.qk, 256, 512)
        w.vbf = Tile(car, 512, BF16)
        w.qkT = Tile(car, 512, BF16)
        w.alias = alias
        w.scT = Tile(car, 128, BF16)
        w.st = Tile(car, 16, F32)
        w.mv = Tile(car, 8, F32)
        w.t1 = Tile(car, 256, F32)
        w.t2 = Tile(car, 256, F32)
        w.rope4 = Tile(car, 512, F32)
        w.mrg = Tile(car, 256, BF16)
        w.mT = Tile(car, 256, BF16)
        wsets.append(w)
    main_end = car.off

    car.off = main_start
    fA = [Tile(car, 2048, BF16) for _ in range(2)]
    fB = [Tile(car, 2048, BF16) for _ in range(2)]
    fT = [Tile(car, 2048, BF16) for _ in range(2)]
    fS = [Tile(car, 2048, BF16) for _ in range(3)]
    fx = [Tile(car, 2048, F32) for _ in range(3)]
    fy = [Tile(car, 2048, F32) for _ in range(3)]
    fn_ = [Tile(car, 2048, F32) for _ in range(2)]
    lnw = Tile(car, 2048, F32)
    lnb = Tile(car, 2048, F32)
    fst = [Tile(car, 32, F32) for _ in range(3)]
    fmv = [Tile(car, 8, F32) for _ in range(3)]

    def _fsize(ap):
        n = 1
        for d in ap.shape[1:]:
            n *= d
        return n

    def mm(out, lhsT, rhs, start=True, stop=True):
        f = lambda e: e.matmul(out, lhsT, rhs, start=start, stop=stop)
        c = max(64.0, 12.0 + _fsize(out) / 2.4)
        if lhsT.dtype == F32:
            c *= 4
        f.cost = c
        return f

    def tr(out, in_):
        f = lambda e: e.transpose(out, in_, IDENT)
        f.cost = 120.0
        return f

    def seq(*fns):
        def f(e):
            r = None
            for g in fns:
                r = g(e)
            return r
        f.cost = sum(getattr(g, "cost", 150.0) for g in fns)
        return f

    def act(out, in_, func, bias=None, scale=None):
        def f(e):
            kw = {}
            if bias is not None:
                kw["bias"] = bias
            if scale is not None:
                kw["scale"] = scale
            return e.activation(out, in_, func, **kw)
        f.cost = 230.0 + 0.9 * _fsize(out)
        return f

    def tt(out, a, b, op):
        f = lambda e: e.tensor_tensor(out, a, b, op)
        f.cost = 150.0 + 1.05 * _fsize(out)
        f.pcost = 250.0 + 2.1 * _fsize(out)
        return f

    def ts(out, a, s1, s2, op0, op1=None):
        if op1 is None:
            f = lambda e: e.tensor_scalar(out, a, s1, None, op0)
        else:
            f = lambda e: e.tensor_scalar(out, a, s1, s2, op0, op1)
        f.cost = 150.0 + 1.05 * _fsize(out)
        return f

    def stt(out, a, s_, b, op0, op1):
        f = lambda e: e.scalar_tensor_tensor(out, a, s_, b, op0, op1)
        f.cost = 150.0 + 1.05 * _fsize(out)
        return f

    def cp(out, in_):
        f = lambda e: e.tensor_copy(out, in_)
        f.cost = 150.0 + 1.0 * _fsize(out)
        f.pcost = 250.0 + 2.0 * _fsize(out)
        return f

    def dmaf(out, in_):
        f = lambda e: e.dma_start(out=out, in_=in_)
        n = 1
        for d in out.shape:
            n *= d
        f.nbytes = n * (4 if (out.dtype == F32 or in_.dtype == F32) else 2)
        return f

    B.dma("c0", dmaf(cst.ap, cst_d), writes=[cst.res])
    B.dma("c1", dmaf(cstb.ap, cstb_d), writes=[cstb.res])
    B.dma("c2", dmaf(gw.ap, gw_d.partition_broadcast(128)), writes=[gw.res])
    B.dma("c3", dmaf(wlr.ap[0:16, :], wlr_d), writes=[wlr.res], q="pool")
    B.dma("c4", dmaf(wlr.ap[16:17, :], blr_d), writes=[wlr.res], q="pool")

    wload_state = {"n": 0}
    last_x = {"op": None}

    def load_unit_ktile(u, kt):
        n = wload_state["n"]
        wload_state["n"] += 1
        dst = wbuf[u % 2].v3(16)[:, kt, :]
        ex_ = [last_x["op"]] if (last_x["op"] is not None and u < 9) else []
        B.dma("wl%d_%d" % (u % 2, kt), dmaf(dst, wu_d[u, kt * 128:(kt + 1) * 128, :]), writes=[wbres[u % 2][kt]], q="pool", extra=ex_)

    for kt in range(16):
        load_unit_ktile(0, kt)

    def proj(u, xt, cols, bank, c0=0):
        wb = wbuf[u % 2].v3(16)
        n = cols[1] - cols[0]
        step = 16 // PSPLIT
        for k0 in range(0, 16, step):
            fns = []
            for kt in range(k0, k0 + step):
                fns.append(mm(banks[bank][:, c0:c0 + n], xt.v3(16)[:, kt, :], wb[:, kt, cols[0]:cols[1]],
                              start=(kt == 0), stop=(kt == 15)))
            B.op("pe", seq(*fns), reads=[xt.res] + wbres[u % 2][k0:k0 + step], writes=[bres[bank]])

    xload_n = {"n": 0}

    def load_x(i):
        n = xload_n["n"]
        xload_n["n"] += 1
        xt = xtb[n % 3]
        last_x["op"] = B.dma("xs%d" % (n % 3), dmaf(xt.ap, xT_d[i]), writes=[xt.res], q="pool")
        return xt

    tile_order = [16] + list(range(16))

    def store_state_prompt(dst_ap, nk, dv):
        B.dma("sp_st", dmaf(dst_ap.rearrange("(k p) v -> p k v", p=128), S.ap[:, 0:nk * dv].rearrange("p (k v) -> p k v", k=nk)),
              reads=[S.res])

    def step_scores(w, i, nk):
        T = 6
        fns = []
        for kt in range(nk):
            fns.append(mm(banks[T][:, 256:384], w.kT.ap[:, kt * 128:(kt + 1) * 128], w.qT.ap[:, kt * 128:(kt + 1) * 128],
                          start=(kt == 0), stop=(kt == nk - 1)))
        B.op("pe", seq(*fns), reads=[w.kT.res, w.qT.res], writes=[bres[T]])

    def step_mask(w, i):
        T = 6
        mask = MASKS if i == 16 else MASKP
        B.op("dve", tt(w.scT.ap, banks[T][:, 256:384], mask, ALU.mult), reads=[bres[T], cstb.res], writes=[w.scT.res])

    def step_oP(w, i, nk, dv, pbank):
        is_s = (i == 16)
        O = 7
        fns = [mm(banks[O][:, 0:dv], w.scT.ap, w.vbf.ap[:, 0:dv], start=True, stop=is_s)]
        rd = [w.scT.res, w.vbf.res]
        if not is_s:
            for kt in range(nk):
                fns.append(mm(banks[O][:, 0:dv], w.qT.ap[:, kt * 128:(kt + 1) * 128], Sbf.ap[:, kt * dv:(kt + 1) * dv],
                              start=False, stop=(kt == nk - 1)))
            rd += [w.qT.res, Sbf.res]
        B.op("pe", seq(*fns), reads=rd, writes=[bres[O]])
        if is_s:
            B.op("act", act(oacc.ap[:, 0:dv], banks[O][:, 0:dv], AF.Copy), reads=[bres[O]], writes=[oacc.res])
            return
        for kt in range(nk):
            pb = pbank[kt]
            B.op("pe", mm(banks[pb][:, 0:dv], w.ktl.ap[:, kt * 128:(kt + 1) * 128], w.vbf.ap[:, 0:dv]),
                 reads=[w.ktl.res, w.vbf.res], writes=[bres[pb]])
            sl = S.ap[:, kt * dv:(kt + 1) * dv]
            B.op("dve", tt(sl, sl, banks[pb][:, 0:dv], ALU.add), reads=[bres[pb]], writes=[S.res])

    def step_scale_state(nk, dv, e_ap):
        for kt in range(nk):
            sl = S.ap[:, kt * dv:(kt + 1) * dv]
            B.op("act", act(Sbf.ap[:, kt * dv:(kt + 1) * dv], sl, AF.Copy, scale=e_ap[kt]), reads=[S.res] + e_ap[2], writes=[Sbf.res])
            B.op("act", act(sl, sl, AF.Copy, scale=e_ap[kt]), reads=[S.res] + e_ap[2], writes=[S.res])

    def sample_seq_load(sq, nk, dv, state_d, slot=None):
        r = sq % 2 if slot is None else slot
        B.dma("s0_%d" % r, dmaf(S0[r].ap[:, 0:nk * dv].rearrange("p (k v) -> p k v", k=nk),
                                state_d.rearrange("(k p) v -> p k v", p=128)), writes=[S0[r].res])
        for kt in range(nk):
            sl = slice(kt * dv, (kt + 1) * dv)
            if nk == 1 and sq % 2 == 1:
                B.op("dve", cp(S0bf[r].ap[:, sl], S0[r].ap[:, sl]), reads=[S0[r].res], writes=[S0bf[r].res])
            elif kt == 0:
                B.op("act", act(S0bf[r].ap[:, sl], S0[r].ap[:, sl], AF.Copy), reads=[S0[r].res], writes=[S0bf[r].res])
            else:
                B.op("dve", cp(S0bf[r].ap[:, sl], S0[r].ap[:, sl]), reads=[S0[r].res], writes=[S0bf[r].res])

    def sample_seq_ops(sq, nk, dv, out_d, e_fn, obank, pbanks, mask_eng="dve", slot=None):
        r = sq % 2 if slot is None else slot
        qv = qTm[r].ap[:, 0:nk * 128].rearrange("p (k t) -> p k t", k=nk)
        qs = qTs.ap[:, 0:nk * 128].rearrange("p (k t) -> p k t", k=nk)
        B.op("pool", cp(qv[:, :, 4 * sq:4 * sq + 4], qs[:, :, 4 * sq:4 * sq + 4]), reads=[qTs.res], writes=[qTm[r].res])
        if mask_eng == "act":
            B.op("act", act(ktm[r].ap[:, 0:nk * 128], kts.ap[:, 0:nk * 128], AF.Copy, scale=ROWSEL[:, sq:sq + 1]),
                 reads=[kts.res, cst.res], writes=[ktm[r].res])
        else:
            B.op("dve", ts(ktm[r].ap[:, 0:nk * 128], kts.ap[:, 0:nk * 128], ROWSEL[:, sq:sq + 1], None, ALU.mult),
                 reads=[kts.res, cst.res], writes=[ktm[r].res])
        fns = []
        for kt in range(nk):
            fns.append(mm(banks[obank][:, 0:dv], qv[:, kt, :], S0bf[r].ap[:, kt * dv:(kt + 1) * dv], start=(kt == 0), stop=(kt == nk - 1)))
        B.op("pe", seq(*fns), reads=[qTm[r].res, S0bf[r].res], writes=[bres[obank]])
        B.op("dve", tt(oacc.ap[:, 0:dv], oacc.ap[:, 0:dv], banks[obank][:, 0:dv], ALU.add), reads=[bres[obank]], writes=[oacc.res])
        for kt in range(nk):
            pb = pbanks[kt]
            B.op("pe", mm(banks[pb][:, 0:dv], ktm[r].ap[:, kt * 128:(kt + 1) * 128], vs.ap[:, 0:dv]),
                 reads=[ktm[r].res, vs.res], writes=[bres[pb]])
        B.op("pool", lambda e: e.memset(qv[:, :, 4 * sq:4 * sq + 4], 0.0), writes=[qTm[r].res])
        for kt in range(nk):
            pb = pbanks[kt]
            sl = S0[r].ap[:, kt * dv:(kt + 1) * dv]
            B.op("dve", tt(sl, sl, banks[pb][:, 0:dv], ALU.add), reads=[bres[pb]], writes=[S0[r].res])
        e_ap = e_fn(sq)
        for kt in range(nk):
            B.op("act", act(Sout[r].ap[:, kt * dv:(kt + 1) * dv], S0[r].ap[:, kt * dv:(kt + 1) * dv], AF.Copy, scale=e_ap[kt]),
                 reads=[S0[r].res] + e_ap[2], writes=[Sout[r].res])
        B.dma("so_%d" % r, dmaf(out_d.rearrange("(k p) v -> p k v", p=128),
                                Sout[r].ap[:, 0:nk * dv].rearrange("p (k v) -> p k v", k=nk)), reads=[Sout[r].res])

    def rstd_from(var_ap, eps_ap, out_ap, w, scale=1.0):
        B.op("act", act(out_ap, var_ap, AF.Ln, bias=eps_ap, scale=scale), reads=[w.mv.res, cst.res], writes=[w.mv.res])
        B.op("act", act(out_ap, out_ap, AF.Exp, scale=-0.5), reads=[w.mv.res], writes=[w.mv.res])

    for w_ in wsets:
        B.op("pool", lambda e, w_=w_: e.memset(w_.lraT.ap, 1.0), writes=[w_.lraT.res])
    B.op("pool", lambda e: e.memset(qTm[0].ap, 0.0), writes=[qTm[0].res])
    B.op("pool", lambda e: e.memset(qTm[1].ap, 0.0), writes=[qTm[1].res])

    def bankset(n):
        return (0, 1) if n % 2 == 0 else (2, 3)

    def pass_arec(u, h):
        nk, dv = 2, 512
        B.op("pool", lambda e: e.memset(S.ap, 0.0), writes=[S.res])
        B.op("pool", lambda e: e.memset(Sbf.ap, 0.0), writes=[Sbf.res])
        G, M, T, O = 4, 5, 6, 7
        xts = {}
        wb = wbuf[u % 2].v3(16)
        for w_ in wsets:
            w_.qT = w_.alias(w_.qkT, 0, 256)
            w_.kT = w_.alias(w_.qkT, 256, 512)
            w_.qtl = w_.alias(w_.qk, 0, 256)
            w_.ktl = w_.alias(w_.qk, 256, 512)

        def L(n):
            xt = xts[n]
            fns = [mm(banks[M][0:16, 0:128], wb[:, kt, 1024:1040], xt.v3(16)[:, kt, :], start=(kt == 0), stop=(kt == 15))
                   for kt in range(16)]
            B.op("pe", seq(*fns), reads=[xt.res] + wbres[u % 2], writes=[bres[M]])

        def A0(n):
            proj(u, xts[n], (0, 512), bankset(n)[0])

        def A1(n):
            proj(u, xts[n], (512, 1024), bankset(n)[1])

        def B1(n):
            w = wsets[n % 2]
            B.op("act", act(w.lraT.ap[0:16, :], banks[M][0:16, 0:128], AF.Copy), reads=[bres[M]], writes=[w.lraT.res])
            B.op("pe", mm(banks[G][:, 0:256], w.lraT.ap[0:17, :], wlr.ap[0:17, h * 256:(h + 1) * 256]),
                 reads=[w.lraT.res, wlr.res], writes=[bres[G]])

        def B2(n):
            w = wsets[n % 2]
            B.op("act", act(w.ex.ap, banks[G][:, 0:256], AF.Exp, scale=-1.0), reads=[bres[G]], writes=[w.ex.res])
            B.op("act", act(w.sp.ap, w.ex.ap, AF.Ln, bias=C_ONE), reads=[w.ex.res, cst.res], writes=[w.sp.res])

        def B3(n):
            w = wsets[n % 2]
            is_s = (tile_order[n] == 16)
            uc = UCS if is_s else UCP
            B.op("pe", mm(banks[G][:, 256:512], uc, w.sp.ap), reads=[w.sp.res, cst.res], writes=[bres[G]])
            if is_s:
                fns = [mm(banks[M][:, 256 + 32 * kt:256 + 32 * kt + 32], w.sp.ap[:, kt * 128:(kt + 1) * 128], SELNEG) for kt in range(2)]
            else:
                fns = [mm(banks[M][:, 256 + 2 * kt:256 + 2 * kt + 2], w.sp.ap[:, kt * 128:(kt + 1) * 128], NEGCOL) for kt in range(2)]
            B.op("pe", seq(*fns), reads=[w.sp.res, cst.res], writes=[bres[M]])

        def B4(n):
            w = wsets[n % 2]
            is_s = (tile_order[n] == 16)
            pa, pv = bankset(n)
            B.op("act", act(w.Eb.ap, banks[G][:, 256:512], AF.Exp, bias=C_LNQ), reads=[bres[G], cst.res], writes=[w.Eb.res])
            B.op("act", act(w.Enb.ap, banks[G][:, 256:512], AF.Exp, scale=-1.0), reads=[bres[G]], writes=[w.Enb.res])
            B.op("dve", tt(w.qtl.ap, banks[pa][:, 0:256], w.Eb.ap, ALU.mult), reads=[bres[pa], w.Eb.res], writes=[w.qtl.res])
            B.op("dve", tt(w.ktl.ap, banks[pa][:, 256:512], w.Enb.ap, ALU.mult), reads=[bres[pa], w.Enb.res], writes=[w.ktl.res])
            if is_s:
                B.op("act", act(esm.ap, banks[M][:, 256:320], AF.Exp), reads=[bres[M]], writes=[esm.res])
            else:
                B.op("act", act(w.ecol.ap[:, 0:4], banks[M][:, 256:260], AF.Exp), reads=[bres[M]], writes=[w.ecol.res])
            B.op("act", act(w.vbf.ap, banks[pv][:, 0:512], AF.Copy), reads=[bres[pv]], writes=[w.vbf.res])

        def Tr(n):
            w = wsets[n % 2]
            tb = banks[T][:, 0:256].bitcast(BF16)
            fns = []
            for kt in range(2):
                fns.append(lambda e, kt=kt, qa=w.qtl.ap: e.transpose(tb[:, kt * 128:(kt + 1) * 128], qa[:, kt * 128:(kt + 1) * 128], IDENT))
                fns.append(lambda e, kt=kt, ka=w.ktl.ap: e.transpose(tb[:, 256 + kt * 128:256 + (kt + 1) * 128], ka[:, kt * 128:(kt + 1) * 128], IDENT))
            B.op("pe", seq(*fns), reads=[w.qtl.res, w.ktl.res, cstb.res], writes=[bres[T]])

        def cpT(n):
            w = wsets[n % 2]
            tb = banks[T][:, 0:256].bitcast(BF16)
            B.op("dve", cp(w.qkT.ap, tb[:, 0:512]), reads=[bres[T]], writes=[w.qkT.res])
            if tile_order[n] == 16:
                B.op("pool", cp(qTs.ap, w.qT.ap), reads=[w.qT.res], writes=[qTs.res])
                B.op("pool", cp(kts.ap, w.ktl.ap), reads=[w.ktl.res], writes=[kts.res])
                B.op("pool", cp(vs.ap, w.vbf.ap), reads=[w.vbf.res], writes=[vs.res])

        def Dn(n):
            w = wsets[n % 2]
            i = tile_order[n]
            if i == 16:
                return
            step_scale_state(nk, dv, (w.ecol.ap[:, 0:1], w.ecol.ap[:, 2:3], [w.ecol.res]))
            finish_o_gla(w, i, banks[O][:, 0:512], bres[O])

        def D2b(n, fb):
            i = tile_order[n]
            if i == 16:
                return
            ef = lambda s_: (esm.ap[:, s_:s_ + 1], esm.ap[:, 32 + s_:33 + s_], [esm.res])
            sample_seq_ops(2 * i, nk, dv, sgs_d[2 * i, h], ef, G, (T, M), slot=i % 2)

        def seq_loads(n):
            i = tile_order[n]
            if i == 16:
                return
            sample_seq_load(2 * i, nk, dv, sgla_d[2 * i, h], slot=i % 2)

        xts[0] = load_x(tile_order[0])
        xts[1] = load_x(tile_order[1])
        L(0); A1(0); B1(0); A0(0); B2(0); B3(0); B4(0)
        for n in range(NT):
            w = wsets[n % 2]
            i = tile_order[n]
            nx = n + 1 < NT
            if n + 2 < NT:
                xts[n + 2] = load_x(tile_order[n + 2])
            if n < 16:
                load_unit_ktile(u + 1, n)
            if n == 2:
                flush_cc()
            Tr(n)
            if nx:
                L(n + 1); A1(n + 1); B1(n + 1)
            cpT(n)
            step_scores(w, i, nk)
            if nx:
                A0(n + 1); B2(n + 1)
            step_mask(w, i)
            if nx:
                B3(n + 1)
            step_oP(w, i, nk, dv, bankset(n))
            if nx:
                B4(n + 1)
            Dn(n)
            if n >= 1:
                D2b(n - 1, bankset(n + 1))
            seq_loads(n)
        D2b(NT - 1, bankset(NT))
        store_state_prompt(sgp_d[h], 2, 512)

    def finish_o_gla(w, i, o_ap, o_res):
        B.op("dve", lambda e: e.bn_stats(w.st.ap[:, 0:6], o_ap), reads=[o_res], writes=[w.st.res])
        B.op("dve", lambda e: e.bn_aggr(w.mv.ap[:, 0:2], w.st.ap[:, 0:6]), reads=[w.st.res], writes=[w.mv.res])
        B.op("dve", stt(w.mv.ap[:, 2:3], w.mv.ap[:, 0:1], w.mv.ap[:, 0:1], w.mv.ap[:, 1:2], ALU.mult, ALU.add),
             reads=[w.mv.res], writes=[w.mv.res])
        rstd_from(w.mv.ap[:, 2:3], C_HEPS, w.mv.ap[:, 3:4], w)
        dst = partA.ap[:, i * 512:(i + 1) * 512]
        B.op("dve", stt(dst, o_ap, w.mv.ap[:, 3:4], gw.ap, ALU.mult, ALU.mult), reads=[o_res, w.mv.res, gw.res], writes=[partA_res[i]])

    def pass_agate(u, h):
        order_g = list(range(16)) + [16]
        ef = lambda s_: (esm.ap[:, s_:s_ + 1], esm.ap[:, 32 + s_:33 + s_], [esm.res])
        xts = {}
        xts[0] = load_x(order_g[0])
        xts[1] = load_x(order_g[1])
        for n, i in enumerate(order_g):
            w = wsets[n % 2]
            pa, pv = 2 * (n % 3), 2 * (n % 3) + 1
            xt = xts[n]
            if n + 2 < NT:
                xts[n + 2] = load_x(order_g[n + 2])
            if n < 16:
                load_unit_ktile(u + 1, n)
            if n == 2:
                flush_cc()
            if n == 0:
                sample_seq_load(1, 2, 512, sgla_d[1, h], slot=0)
            if n + 1 < 16:
                sample_seq_load(2 * (n + 1) + 1, 2, 512, sgla_d[2 * (n + 1) + 1, h], slot=(n + 1) % 2)
            if i == 16:
                finish_o_gla(wsets[0], 16, oacc.ap, oacc.res)
            proj(u, xt, (0, 512), pa)
            proj(u, xt, (512, 1024), pv)
            B.op("act", act(w.sz.ap, banks[pa][:, 0:512], AF.Exp, scale=-1.0), reads=[bres[pa]], writes=[w.sz.res])
            B.op("act", act(w.sz.ap, w.sz.ap, AF.Ln, bias=C_ONE), reads=[cst.res], writes=[w.sz.res])
            B.op("act", act(w.sg.ap, banks[pv][:, 0:512], AF.Exp, scale=-1.0), reads=[bres[pv]], writes=[w.sg.res])
            B.op("act", act(w.sg.ap, w.sg.ap, AF.Ln, bias=C_ONE), reads=[cst.res], writes=[w.sg.res])
            B.op("pool", tt(w.sz.ap, w.sz.ap, w.sg.ap, ALU.add), reads=[w.sg.res], writes=[w.sz.res])
            B.op("act", act(w.sz.ap, w.sz.ap, AF.Exp, scale=-1.0), reads=[], writes=[w.sz.res])
            B.op("dve", tt(w.sz.ap, banks[pa][:, 0:512], w.sz.ap, ALU.mult), reads=[bres[pa]], writes=[w.sz.res])
            dst = partA.ap[:, i * 512:(i + 1) * 512]
            B.op("pool", tt(dst, dst, w.sz.ap, ALU.mult), reads=[w.sz.res], writes=[partA_res[i]])
            if i != 16:
                sample_seq_ops(2 * i + 1, 2, 512, sgs_d[2 * i + 1, h], ef, 6, (7, 6), slot=i % 2)

    def tile_cols(i):
        if i < 16:
            return [(0, 128, (i // 8) * 1088 + (i % 8) * 128)]
        return [(0, 64, 1024), (64, 64, 1088 + 1024)]

    def pass_ret(u, j):
        nk, dv = 1, 256
        h = j // 2
        pcol = (j % 2) * 256
        finish_o_ret.pcol = pcol
        B.op("pool", lambda e: e.memset(S.ap, 0.0), writes=[S.res])
        B.op("pool", lambda e: e.memset(Sbf.ap, 0.0), writes=[Sbf.res])
        B.dma("c5", dmaf(rwt.ap, rw_d[:, j * 256:(j + 1) * 256].partition_broadcast(128)), writes=[rwt.res])
        B.dma("c6", dmaf(rbt.ap, rb_d[:, j * 256:(j + 1) * 256].partition_broadcast(128)), writes=[rbt.res])
        T, O = 6, 7
        rt = RT[:, 8 * j:8 * j + 8]
        xts = {}
        for w_ in wsets:
            w_.qT = w_.alias(w_.qkT, 0, 128)
            w_.kT = w_.alias(w_.qkT, 128, 256)
            w_.qtl = w_.alias(w_.qk, 0, 128)
            w_.ktl = w_.alias(w_.qk, 128, 256)

        def A0(n):
            proj(u, xts[n], (0, 512), bankset(n)[0])
            wv = wsets[n % 2]
            B.dma("rp%d" % (n % 2), dmaf(wv.rope4.ap, rope_d[j, tile_order[n]]), writes=[wv.rope4.res])

        def A1(n):
            proj(u, xts[n], (512, 1024), bankset(n)[1])

        def Bq(n):
            w = wsets[n % 2]
            is_s = (tile_order[n] == 16)
            pa = bankset(n)[0]
            C4, S4 = w.rope4.ap[:, 0:256], w.rope4.ap[:, 256:512]
            X = banks[pa][:, 0:256]
            X4 = X.rearrange("p (a b c) -> p a b c", a=2, b=2)
            S44 = S4.rearrange("p (a b c) -> p a b c", a=2, b=2)
            t24 = w.t2.ap.rearrange("p (a b c) -> p a b c", a=2, b=2)
            B.op("dve", tt(w.t1.ap, X, C4, ALU.mult), reads=[bres[pa], w.rope4.res], writes=[w.t1.res])
            B.op("dve", tt(t24[:, :, 0, :], X4[:, :, 1, :], S44[:, :, 0, :], ALU.mult), reads=[bres[pa], w.rope4.res], writes=[w.t2.res])
            B.op("dve", tt(t24[:, :, 1, :], X4[:, :, 0, :], S44[:, :, 1, :], ALU.mult), reads=[bres[pa], w.rope4.res], writes=[w.t2.res])
            B.op("pool", tt(w.qk.ap[:, 0:256], w.t1.ap, w.t2.ap, ALU.add), reads=[w.t1.res, w.t2.res], writes=[w.qk.res])
            B.op("act", act(w.vbf.ap[:, 0:256], banks[pa][:, 256:512], AF.Copy), reads=[bres[pa]], writes=[w.vbf.res])

        def Tr(n):
            w = wsets[n % 2]
            tb = banks[T][:, 0:256].bitcast(BF16)
            B.op("pe", seq(lambda e, qa=w.qtl.ap: e.transpose(tb[:, 0:128], qa[:, 0:128], IDENT),
                           lambda e, ka=w.ktl.ap: e.transpose(tb[:, 128:256], ka[:, 0:128], IDENT)),
                 reads=[w.qtl.res, w.ktl.res, cstb.res], writes=[bres[T]])

        def cpT(n):
            w = wsets[n % 2]
            tb = banks[T][:, 0:256].bitcast(BF16)
            B.op("dve", cp(w.qkT.ap[:, 0:256], tb[:, 0:256]), reads=[bres[T]], writes=[w.qkT.res])
            if tile_order[n] == 16:
                B.op("pool", cp(qTs.ap[:, 0:128], w.qT.ap[:, 0:128]), reads=[w.qT.res], writes=[qTs.res])
                B.op("pool", cp(kts.ap[:, 0:128], w.ktl.ap[:, 0:128]), reads=[w.ktl.res], writes=[kts.res])
                B.op("pool", cp(vs.ap[:, 0:256], w.vbf.ap[:, 0:256]), reads=[w.vbf.res], writes=[vs.res])

        def gates(n):
            w = wsets[n % 2]
            i = tile_order[n]
            pv = bankset(n)[1]
            B.op("act", act(w.sg.ap, banks[pv][:, 0:512], AF.Exp, scale=-1.0), reads=[bres[pv]], writes=[w.sg.res])
            B.op("act", act(w.sg.ap, w.sg.ap, AF.Ln, bias=C_ONE), reads=[cst.res], writes=[w.sg.res])
            B.op("act", act(w.sg.ap, w.sg.ap, AF.Exp, scale=-1.0), reads=[], writes=[w.sg.res])
            gdst = gs if i == 16 else w.sz
            B.op("dve", tt(w.sz.ap[:, 0:256], banks[pv][:, 0:256], w.sg.ap[:, 0:256], ALU.mult), reads=[bres[pv], w.sg.res], writes=[w.sz.res])
            B.op("dve", tt(gdst.ap[:, 0:256], w.sz.ap[:, 0:256], w.sg.ap[:, 256:512], ALU.mult), reads=[w.sz.res, w.sg.res], writes=[gdst.res])

        def D1(n):
            w = wsets[n % 2]
            i = tile_order[n]
            if i == 16:
                return
            step_scale_state(nk, dv, (rt[:, 4:5], None, [cst.res]))
            finish_o_ret(w, i, banks[O][:, 0:256], bres[O], w.sz)

        def D2a(n):
            i = tile_order[n]
            if i == 16:
                return
            merged_out(wsets[n % 2], i)

        def D2b(n, fb):
            i = tile_order[n]
            if i == 16:
                return
            ef = lambda s_: (rt[:, 5:6], None, [cst.res])
            sample_seq_ops(2 * i, nk, dv, srs_d[2 * i, j], ef, 5, (4,), mask_eng="act")
            sample_seq_ops(2 * i + 1, nk, dv, srs_d[2 * i + 1, j], ef, T, (fb[0],), mask_eng="act")

        def seq_loads(n):
            i = tile_order[n]
            if i == 16:
                return
            for sq in (2 * i, 2 * i + 1):
                sample_seq_load(sq, nk, dv, sret_d[sq, j])

        xts[0] = load_x(tile_order[0])
        xts[1] = load_x(tile_order[1])
        A0(0); A1(0); Bq(0)
        for n in range(NT):
            w = wsets[n % 2]
            i = tile_order[n]
            nx = n + 1 < NT
            if n + 2 < NT:
                xts[n + 2] = load_x(tile_order[n + 2])
            if n < 16:
                load_unit_ktile(u + 1, n)
            if n == 2:
                flush_cc()
            Tr(n)
            gates(n)
            if nx:
                A0(n + 1)
            cpT(n)
            if nx:
                Bq(n + 1)
            if n >= 1:
                D2a(n - 1)
            step_scores(w, i, nk)
            if nx:
                A1(n + 1)
            step_mask(w, i)
            step_oP(w, i, nk, dv, bankset(n))
            D1(n)
            if n >= 1:
                D2b(n - 1, bankset(n + 1))
            seq_loads(n)
        D2a(NT - 1)
        D2b(NT - 1, bankset(NT))
        w0 = wsets[0]
        finish_o_ret(w0, 16, oacc.ap[:, 0:256], oacc.res, gs)
        merged_out(w0, 16)
        store_state_prompt(srp_d[j], 1, 256)
        tok = B.dma("ibst", dmaf(ib_d[j].rearrange("(k p) c -> p k c", p=128), mTb.v3(2)), reads=[mTb.res])
        pending_cc.append((j, tok))

    def finish_o_ret(w, i, o_ap, o_res, g):
        pcol = finish_o_ret.pcol
        B.op("dve", lambda e: e.bn_stats(w.st.ap[:, 0:6], o_ap), reads=[o_res], writes=[w.st.res])
        B.op("dve", lambda e: e.bn_aggr(w.mv.ap[:, 0:2], w.st.ap[:, 0:6]), reads=[w.st.res], writes=[w.mv.res])
        rstd_from(w.mv.ap[:, 1:2], C_HEPS, w.mv.ap[:, 3:4], w)
        B.op("dve", stt(w.on.ap, o_ap, w.mv.ap[:, 0:1], rwt.ap, ALU.subtract, ALU.mult),
             reads=[o_res, w.mv.res, rwt.res], writes=[w.on.res])
        B.op("dve", stt(w.on.ap, w.on.ap, w.mv.ap[:, 3:4], rbt.ap, ALU.mult, ALU.add),
             reads=[w.mv.res, rbt.res], writes=[w.on.res])
        B.op("dve", tt(w.on.ap, w.on.ap, g.ap[:, 0:256], ALU.mult), reads=[g.res], writes=[w.on.res])
        pa_sl = partA.ap[:, i * 512 + pcol:i * 512 + pcol + 256]
        B.op("dve", tt(w.mrg.ap, w.on.ap, pa_sl, ALU.add), reads=[w.on.res, partA_res[i]], writes=[w.mrg.res])

    def merged_out(w, i):
        T = 4
        tb = banks[T][:, 0:256].bitcast(BF16)
        B.op("pe", seq(lambda e, ma=w.mrg.ap: e.transpose(tb[:, 0:128], ma[:, 0:128], IDENT),
                       lambda e, ma=w.mrg.ap: e.transpose(tb[:, 128:256], ma[:, 128:256], IDENT)),
             reads=[w.mrg.res, cstb.res], writes=[bres[T]])
        mv3 = mTb.v3(2)
        tb3 = tb[:, 0:256].rearrange("p (k t) -> p k t", k=2)
        for (t0, nt_, c0) in tile_cols(i):
            B.op("act", act(mv3[:, :, c0:c0 + nt_], tb3[:, :, t0:t0 + nt_], AF.Copy), reads=[bres[T]], writes=[mTb.res])

    import os
    npass = int(os.environ.get("K_NPASS", "8"))
    for u in range(npass):
        kind = KINDS[u]
        if kind == "arec":
            pass_arec(u, u // 4)
        elif kind == "agate":
            pass_agate(u, u // 4)
        else:
            pass_ret(u, (u // 4) * 2 + (u % 4 - 2))

    kfinal = int(os.environ.get("K_FINAL", "2"))
    sinks0 = B.sinks()
    for kt in range(16 if kfinal >= 1 else 0):
        load_unit_ktile(9, kt)
    flush_cc()
    bar_t = Tile(Carver(main_end), 16, F32)
    bar = B.op("pool", lambda e: e.memset(bar_t.ap, 0.0), extra=sinks0)
    B.extra = [bar]
    B.dma("c7", dmaf(lnw.ap, lnw_d.partition_broadcast(128)), writes=[lnw.res])
    B.dma("c8", dmaf(lnb.ap, lnb_d.partition_broadcast(128)), writes=[lnb.res])
    KSPL = ((0, 12), (12, 16))
    fres = {nm: [[Res(), Res()] for _ in range(3)] for nm in ("A", "B", "T", "S")}

    def floads(m):
        nt_ = 128 if m < 8 else 64
        r = m % 2
        a3, b3 = fA[r].v3(16), fB[r].v3(16)
        for part, (k0, k1) in enumerate(KSPL):
            ex_ = [cc_toks[j] for j in sorted(cc_toks) if (j < 3) == (part == 0)]
            rows = slice(k0 * 128, k1 * 128)
            B.dma("fa%d_%d" % (r, part), dmaf(a3[:, k0:k1, 0:nt_], ob_all[rows, m * 128:m * 128 + nt_].rearrange("(k p) c -> p k c", p=128)),
                  writes=[fres["A"][r][part]], extra=ex_)
            B.dma("fb%d_%d" % (r, part), dmaf(b3[:, k0:k1, 0:nt_], ob_all[rows, 1088 + m * 128:1088 + m * 128 + nt_].rearrange("(k p) c -> p k c", p=128)),
                  writes=[fres["B"][r][part]], extra=ex_)
        q = m % 3
        B.dma("fx%d" % q, dmaf(fx[q].ap[0:nt_, :], xres_d[m * 128:m * 128 + nt_, :]), writes=[fx[q].res])

    def fselect(m):
        nt_ = 128 if m < 8 else 64
        r = m % 2
        q = m % 3
        a3, b3, t3, s3 = fA[r].v3(16), fB[r].v3(16), fT[r].v3(16), fS[q].v3(16)
        for part, (k0, k1) in enumerate(KSPL):
            B.op("act", act(t3[:, k0:k1, 0:nt_], a3[:, k0:k1, 0:nt_], AF.Copy, scale=C_H0), reads=[fres["A"][r][part], cst.res], writes=[fres["T"][r][part]])
            B.op("dve", stt(s3[:, k0:k1, 0:nt_], b3[:, k0:k1, 0:nt_], C_H1, t3[:, k0:k1, 0:nt_], ALU.mult, ALU.add),
                 reads=[fres["B"][r][part], fres["T"][r][part], cst.res], writes=[fres["S"][q][part]])

    if kfinal >= 2:
        floads(0)
    for m in range(9 if kfinal >= 2 else 0):
        nt_ = 128 if m < 8 else 64
        r = m % 2
        q = m % 3
        s3 = fS[q].v3(16)
        if m + 1 < 9:
            floads(m + 1)
        if m == 0:
            fselect(0)
        if m + 1 < 9:
            fselect(m + 1)
        for blk in range(4):
            bk = 4 * r + blk
            wb = wbuf[blk // 2].v3(16)
            for part, (k0, k1) in enumerate(KSPL):
                fns = [mm(banks[bk][0:nt_, 0:512], s3[:, kt, 0:nt_], wb[:, kt, (blk % 2) * 512:(blk % 2) * 512 + 512],
                          start=(kt == 0), stop=(kt == 15)) for kt in range(k0, k1)]
                B.op("pe", seq(*fns), reads=[fres["S"][q][part]] + wbres[blk // 2][k0:k1], writes=[bres[bk]])
        for blk in range(4):
            bk = 4 * r + blk
            ysl = fy[q].ap[0:nt_, blk * 512:(blk + 1) * 512]
            B.op("dve", stt(ysl, fx[q].ap[0:nt_, blk * 512:(blk + 1) * 512], ALPHA, banks[bk][0:nt_, 0:512], ALU.mult, ALU.add),
                 reads=[fx[q].res, bres[bk]], writes=[fy[q].res])
            B.op("dve", lambda e, ysl=ysl, blk=blk, q=q, nt_=nt_: e.bn_stats(fst[q].ap[0:nt_, blk * 6:(blk + 1) * 6], ysl),
                 reads=[fy[q].res], writes=[fst[q].res])
        mv = fmv[q].ap
        B.op("dve", lambda e, q=q, nt_=nt_: e.bn_aggr(fmv[q].ap[0:nt_, 0:2], fst[q].ap[0:nt_, 0:24]), reads=[fst[q].res], writes=[fmv[q].res])
        B.op("act", act(mv[0:nt_, 3:4], mv[0:nt_, 1:2], AF.Ln, bias=C_LEPS[0:nt_, :]), reads=[fmv[q].res, cst.res], writes=[fmv[q].res])
        B.op("act", act(mv[0:nt_, 3:4], mv[0:nt_, 3:4], AF.Exp, scale=-0.5), reads=[fmv[q].res], writes=[fmv[q].res])
        B.op("dve", stt(mv[0:nt_, 4:5], mv[0:nt_, 0:1], -1.0, mv[0:nt_, 3:4], ALU.mult, ALU.mult), reads=[fmv[q].res], writes=[fmv[q].res])
        B.op("act", act(fn_[r].ap[0:nt_, :], fy[q].ap[0:nt_, :], AF.Identity, bias=mv[0:nt_, 4:5], scale=mv[0:nt_, 3:4]),
             reads=[fy[q].res, fmv[q].res], writes=[fn_[r].res])
        B.op("dve", tt(fn_[r].ap[0:nt_, :], fn_[r].ap[0:nt_, :], lnw.ap[0:nt_, :], ALU.mult), reads=[lnw.res], writes=[fn_[r].res])
        B.op("pool", tt(fn_[r].ap[0:nt_, :], fn_[r].ap[0:nt_, :], lnb.ap[0:nt_, :], ALU.add), reads=[lnb.res], writes=[fn_[r].res])
        B.dma("yo%d" % r, dmaf(y_d[m * 128:m * 128 + nt_, :], fn_[r].ap[0:nt_, :]), reads=[fn_[r].res])

    order = B.schedule()
    if os.environ.get('K_VERBOSE'):
        print('estimated makespan us', B.makespan / 1000.0, {k: len(v) for k, v in order.items()}, {k: round(sum(o.issue for o in v) / 1000.0) for k, v in order.items()})
    sems = {}
    for e in ("pe", "act", "dve", "pool"):
        sems[e] = es.enter_context(nc.semaphore("s_" + e))
    cnt = {}
    for st in Builder.STREAMS:
        for o in order[st]:
            if o.kind == "eng":
                cnt[st] = cnt.get(st, 0) + 1
                o.tok = (st, cnt[st])
            elif o.kind == "dma":
                k = "d:" + o.key
                if k not in sems:
                    sems[k] = es.enter_context(nc.semaphore("d_" + o.key))
                cnt[k] = cnt.get(k, 0) + 16
                o.tok = (k, cnt[k])
            else:
                k = "c:" + o.key
                sems[k] = es.enter_context(nc.semaphore("c_" + o.key))
                o.tok = (k, 1)
    final_toks = [o.tok for o in B.sinks()]

    def emit(name, eng, final=False):
        waited = {}

        def wait(tok):
            sname, val = tok
            if waited.get(sname, 0) >= val:
                return
            waited[sname] = val
            eng.wait_ge(sems[sname], val)

        for o in order[name]:
            toks = set()
            for d in o.raw:
                if d.kind == "eng" and d.stream == name and name == "pe":
                    continue
                toks.add(d.tok)
            for d in o.war:
                if d.kind == "eng" and d.stream == name and (name == "pe" or not SAMEWAR):
                    continue
                toks.add(d.tok)
            for t in sorted(toks):
                wait(t)
            ins = o.fn(eng)
            ins.then_inc(sems[o.tok[0]], 16 if o.kind == "dma" else 1)
        if final:
            for t in sorted(final_toks):
                wait(t)

    with nc.Block() as block:
        @block.sync
        def _(e):
            emit("sp", e, final=True)

        @block.tensor
        def _(e):
            emit("pe", e)

        @block.scalar
        def _(e):
            emit("act", e)

        @block.vector
        def _(e):
            emit("dve", e)

        @block.gpsimd
        def _(e):
            emit("pool", e)

    es.close()
    nc._B = B
    nc._order = order
    return nc


_PERM = np.concatenate([np.arange(0, 128, 2), np.arange(1, 128, 2)])
_OFF = dict(qA=0, kA=1024, vA=2048, zA=4096, lr=6144, qB=6160, kB=7184, vB=8208, zB=10256, ga=12304, gb=14352)


def _unit_cols(hf):
    units = []
    for h in range(2):
        H = 2 * hf + h
        ar = np.concatenate([_OFF["qA"] + H * 256 + np.arange(256), _OFF["kA"] + H * 256 + np.arange(256),
                             _OFF["vA"] + H * 512 + np.arange(512), _OFF["lr"] + np.arange(16)])
        ag = np.concatenate([_OFF["zA"] + H * 512 + np.arange(512), _OFF["ga"] + H * 512 + np.arange(512)])
        units += [ar, ag]
        for jj in range(2):
            J = 4 * hf + 2 * h + jj
            rt = np.concatenate([_OFF["qB"] + J * 128 + _PERM, _OFF["kB"] + J * 128 + _PERM,
                                 _OFF["vB"] + J * 256 + np.arange(256), _OFF["zB"] + J * 256 + np.arange(256),
                                 _OFF["gb"] + J * 256 + np.arange(256)])
            units.append(rt)
    return units


def _consts(hf):
    c = np.zeros((128, 1024), np.float32)
    s = np.arange(128)[:, None]
    t = np.arange(128)[None, :]
    c[:, 0:128] = np.where(s <= t, -1.0 / 16, 0.0)
    same = (s // 4) == (t // 4)
    c[:, 128:256] = np.where(same & (s <= t), -1.0 / 16, 0.0)
    q = np.arange(32)[None, :]
    c[:, 256:288] = ((s // 4) == q).astype(np.float32)
    c[:, 288:320] = c[:, 256:288] * (-1.0 / 16)
    c[:, 320:322] = -1.0 / 16
    c[:, 384:512] = 1.0
    tt_ = np.arange(128, dtype=np.float64)
    for j in range(4):
        J = 4 * hf + j
        g = 1.0 - 2.0 ** (-5.0 - J)
        lg = math.log(g)
        c[:, 512 + 8 * j + 0] = np.exp(lg * (tt_ + 1))
        c[:, 512 + 8 * j + 1] = np.exp(-lg * (tt_ + 1)) * 128.0 ** -0.5
        c[:, 512 + 8 * j + 2] = np.exp(lg * (tt_ % 4 + 1))
        c[:, 512 + 8 * j + 3] = np.exp(-lg * (tt_ % 4 + 1)) * 128.0 ** -0.5
        c[:, 512 + 8 * j + 4] = g ** 128
        c[:, 512 + 8 * j + 5] = g ** 4
    c[:, 544] = math.log(1.0 / 16)
    c[:, 545] = HN_EPS
    c[:, 546] = LN_EPS
    c[:, 547] = 1.0
    c[:, 548] = 1.0 - hf
    c[:, 549] = float(hf)
    import ml_dtypes
    cb = np.zeros((128, 512), np.float32)
    cb[:, 0:128] = np.eye(128)
    cb[:, 128:256] = (s <= t)
    cb[:, 256:384] = same & (s <= t)
    return c, cb.astype(ml_dtypes.bfloat16)


def _rope_tables(hf):
    inv = (1.0 / (10000.0 ** np.linspace(0.0, 1.0, 64, dtype=np.float32))).astype(np.float32)
    out = np.zeros((4, NT, 128, 512), np.float32)
    for i in range(NT):
        if i < 16:
            pos = (i * 128 + np.arange(128)).astype(np.float32)
            tau = np.arange(128, dtype=np.float64)
        else:
            pos = (16384 + np.arange(128) % 4).astype(np.float32)
            tau = (np.arange(128) % 4).astype(np.float64)
        ang = (pos[:, None] * inv[None, :]).astype(np.float32)
        cs, sn = np.cos(ang.astype(np.float64)), np.sin(ang.astype(np.float64))
        cos2 = np.concatenate([cs, cs], axis=1)
        sin2 = np.concatenate([-sn, sn], axis=1)
        for j in range(4):
            J = 4 * hf + j
            lg = math.log(1.0 - 2.0 ** (-5.0 - J))
            ebq = np.exp(lg * (tau + 1))[:, None]
            ebk = (np.exp(-lg * (tau + 1)) * 128.0 ** -0.5)[:, None]
            out[j, i, :, 0:128] = ebq * cos2
            out[j, i, :, 128:256] = ebk * cos2
            out[j, i, :, 256:384] = ebq * sin2
            out[j, i, :, 384:512] = ebk * sin2
    return out


_NC_CACHE = {}


def kernel(x_prompt, x_sample, state_gla, state_ret, w_in, w_lr, b_lr, gla_norm_w,
           ret_norm_w, ret_norm_b, w_out, ln_w, ln_b):
    f32 = np.float32
    x_prompt = np.asarray(x_prompt, f32)
    x_sample = np.asarray(x_sample, f32)
    state_gla = np.asarray(state_gla, f32)
    state_ret = np.asarray(state_ret, f32)
    w_in = np.asarray(w_in, f32)[0]
    w_lr = np.asarray(w_lr, f32)[0]
    b_lr = np.asarray(b_lr, f32)[0]
    gla_norm_w = np.asarray(gla_norm_w, f32)
    ret_norm_w = np.asarray(ret_norm_w, f32)[0]
    ret_norm_b = np.asarray(ret_norm_b, f32)[0]
    w_out = np.asarray(w_out, f32)[0]
    ln_w = np.asarray(ln_w, f32)
    ln_b = np.asarray(ln_b, f32)

    if "nc" not in _NC_CACHE:
        _NC_CACHE["nc"] = build_program()
    nc = _NC_CACHE["nc"]
    ropes = [_rope_tables(0), _rope_tables(1)]
    w_out_p = np.ascontiguousarray(w_out.reshape(2, 4, 256, D).transpose(1, 0, 2, 3).reshape(D, D))

    in_maps = []
    for c in range(8):
        b, hf = c // 2, c % 2
        xs = x_sample[32 * b:32 * b + 32].reshape(128, D)
        X = np.concatenate([x_prompt[b], xs], axis=0)
        xT = X.reshape(NT, 128, 16, 128).transpose(0, 3, 2, 1)
        xT = np.ascontiguousarray(xT).reshape(NT, 128, 2048)
        xres = np.concatenate([x_prompt[b, hf * 1024:(hf + 1) * 1024], xs[hf * 64:(hf + 1) * 64]], axis=0)
        wu = np.zeros((NU, D, WC), f32)
        for u, cols in enumerate(_unit_cols(hf)):
            wu[u, :, :len(cols)] = w_in[:, cols]
        wu[8, :, :1024] = w_out_p[:, 0:1024]
        wu[9, :, :1024] = w_out_p[:, 1024:2048]
        cst, cstb = _consts(hf)
        H0, J0 = 2 * hf, 4 * hf
        in_maps.append({
            "xT": xT, "xres": np.ascontiguousarray(xres), "wu": wu,
            "wlr": np.ascontiguousarray(w_lr[:, H0 * 256:H0 * 256 + 512]),
            "blr": np.ascontiguousarray(b_lr[None, H0 * 256:H0 * 256 + 512]),
            "gw": np.ascontiguousarray(gla_norm_w.reshape(1, 512)),
            "rw": np.ascontiguousarray(ret_norm_w[None, J0 * 256:J0 * 256 + 1024]),
            "rb": np.ascontiguousarray(ret_norm_b[None, J0 * 256:J0 * 256 + 1024]),
            "lnw": np.ascontiguousarray(ln_w.reshape(1, D)), "lnb": np.ascontiguousarray(ln_b.reshape(1, D)),
            "sgla": np.ascontiguousarray(state_gla[0, 32 * b:32 * b + 32, H0:H0 + 2]),
            "sret": np.ascontiguousarray(state_ret[0, 32 * b:32 * b + 32, J0:J0 + 4][:, :, _PERM, :]),
            "rope": ropes[hf], "cst": cst, "cstb": cstb,
        })

    res = run_bass_kernel_spmd(nc, in_maps, core_ids=list(range(8)))
    R = res.results

    y_p = np.zeros((4, 2048, D), f32)
    y_s = np.zeros((128, 4, D), f32)
    g_p = np.zeros((1, 4, 4, 256, 512), f32)
    r_p = np.zeros((1, 4, 8, 128, 256), f32)
    g_s = np.zeros((1, 128, 4, 256, 512), f32)
    r_s = np.zeros((1, 128, 8, 128, 256), f32)
    inv = np.argsort(_PERM)
    for c in range(8):
        b, hf = c // 2, c % 2
        H0, J0 = 2 * hf, 4 * hf
        y = np.asarray(R[c]["y"], f32)
        y_p[b, hf * 1024:(hf + 1) * 1024] = y[0:1024]
        y_s[32 * b + 16 * hf:32 * b + 16 * hf + 16] = y[1024:1088].reshape(16, 4, D)
        g_p[0, b, H0:H0 + 2] = np.asarray(R[c]["sgp"], f32)
        r_p[0, b, J0:J0 + 4] = np.asarray(R[c]["srp"], f32)[:, inv, :]
        g_s[0, 32 * b:32 * b + 32, H0:H0 + 2] = np.asarray(R[c]["sgs"], f32)
        r_s[0, 32 * b:32 * b + 32, J0:J0 + 4] = np.asarray(R[c]["srs"], f32)[:, :, inv, :]
    return (y_p, y_s, g_p, r_p, g_s, r_s)
```

```python
import math
from contextlib import ExitStack

import numpy as np
import concourse.bass as bass
import concourse.mybir as mybir
from concourse.bass_utils import run_bass_kernel_spmd

F32 = mybir.dt.float32
BF16 = mybir.dt.bfloat16
AF = mybir.ActivationFunctionType
ALU = mybir.AluOpType

D = 2048
NT = 17
NSEQ = 32
WC = 1040
NU = 10
NCOL = 2176
LN_EPS = 1e-5
HN_EPS = 1e-6
ALPHA = 2.0 ** 0.25
KINDS = ["arec", "agate", "ret", "ret", "arec", "agate", "ret", "ret"]


import os as _os
HOP = float(_os.environ.get('K_HOP', '400'))
CSC = float(_os.environ.get('K_CSC', '1.4'))
PEPRIO = int(_os.environ.get('K_PEPRIO', '2'))
PSPLIT = int(_os.environ.get('K_PSPLIT', '4'))
DMABW = float(_os.environ.get('K_DMABW', '150'))
SAMEWAR = int(_os.environ.get('K_SAMEWAR', '1'))


class Res:
    __slots__ = ("w", "r", "psum")

    def __init__(self, psum=False):
        self.w = None
        self.r = []
        self.psum = psum


class Op:
    __slots__ = ("stream", "fn", "kind", "key", "raw", "war", "idx", "issue", "lat", "ready_t", "finish", "tok", "nusers", "prio")

    def __init__(self, stream, fn, kind, key, raw, war, idx, issue, lat):
        self.stream, self.fn, self.kind, self.key = stream, fn, kind, key
        self.raw, self.war, self.idx, self.issue, self.lat = raw, war, idx, issue, lat
        self.ready_t = 0.0
        self.finish = 0.0
        self.tok = None
        self.nusers = 0
        self.prio = 1


class Builder:
    STREAMS = ("pe", "act", "dve", "pool", "sp")

    def __init__(self):
        self.all = []
        self.extra = []

    def _deps(self, reads, writes, extra):
        raw, war = set(extra), set()
        raw.update(self.extra)
        for r in reads:
            if r.w is not None:
                raw.add(r.w)
            if r.psum:
                war.update(r.r)
        for w in writes:
            if w.w is not None:
                raw.add(w.w)
            war.update(w.r)
        war -= raw
        return raw, war

    def _add(self, op, reads, writes):
        for d in op.raw:
            d.nusers += 1
        for d in op.war:
            d.nusers += 1
        self.all.append(op)
        for r in reads:
            r.r.append(op)
        for w in writes:
            w.w = op
            w.r = []
        return op

    def op(self, eng, fn, reads=(), writes=(), extra=(), special=None, cost=None):
        raw, war = self._deps(reads, writes, extra)
        c = cost if cost is not None else getattr(fn, "cost", 300.0)
        if eng == "pool" and cost is None:
            c = getattr(fn, "pcost", 2.0 * c)
        if eng != "pe":
            c *= CSC
        kind = "cc" if special is not None else "eng"
        o = Op(eng, fn, kind, special, raw, war, len(self.all), c, HOP if special is None else 30000.0)
        return self._add(o, reads, writes)

    def dma(self, key, fn, reads=(), writes=(), extra=(), q="sp"):
        raw, war = self._deps(reads, writes, extra)
        nbytes = getattr(fn, "nbytes", 65536)
        o = Op(q, fn, "dma", key, raw, war, len(self.all), 100.0 if q == "sp" else 700.0, 2500.0 + nbytes / DMABW)
        return self._add(o, reads, writes)

    def sinks(self):
        return [o for o in self.all if o.nusers == 0]

    def schedule(self):
        users = {}
        nd = {}
        for o in self.all:
            deps = o.raw | o.war
            nd[o] = len(deps)
            for d in deps:
                users.setdefault(d, []).append(o)
                if o.stream == "pe" and PEPRIO:
                    d.prio = 0
        if PEPRIO == 2:
            cpl = {}
            for o in reversed(self.all):
                m = 0.0
                for u in users.get(o, ()):
                    if cpl[u] > m:
                        m = cpl[u]
                cpl[o] = m + o.issue + o.lat
                o.prio = -cpl[o]
        ready = {st: [] for st in self.STREAMS}
        free = {st: 0.0 for st in self.STREAMS}
        order = {st: [] for st in self.STREAMS}
        for o in self.all:
            if nd[o] == 0:
                ready[o.stream].append(o)
        remaining = len(self.all)
        while remaining:
            best = None
            for st in self.STREAMS:
                lst = ready[st]
                if not lst:
                    continue
                f = free[st]
                cand = min(lst, key=lambda o: (max(o.ready_t, f), o.prio, o.idx))
                t = max(cand.ready_t, f)
                if best is None or (t, cand.idx) < (best[0], best[1].idx):
                    best = (t, cand)
            t, o = best
            ready[o.stream].remove(o)
            free[o.stream] = t + o.issue
            o.finish = t + o.issue + o.lat
            order[o.stream].append(o)
            for u in users.get(o, ()):
                nd[u] -= 1
                if o.finish > u.ready_t:
                    u.ready_t = o.finish
                if nd[u] == 0:
                    ready[u.stream].append(u)
            remaining -= 1
        self.makespan = max(free.values())
        return order


def build_program():
    nc = bass.Bass("TRN2", target_bir_lowering=False)
    B = Builder()

    def din(name, shape, dt=F32):
        return nc.dram_tensor(name, list(shape), dt, kind="ExternalInput").ap()

    def dout(name, shape, dt=F32):
        return nc.dram_tensor(name, list(shape), dt, kind="ExternalOutput").ap()

    xT_d = din("xT", [NT, 128, 16 * 128])
    xres_d = din("xres", [1088, D])
    wu_d = din("wu", [NU, D, WC])
    wlr_d = din("wlr", [16, 512])
    blr_d = din("blr", [1, 512])
    gw_d = din("gw", [1, 512])
    rw_d = din("rw", [1, 1024])
    rb_d = din("rb", [1, 1024])
    lnw_d = din("lnw", [1, D])
    lnb_d = din("lnb", [1, D])
    sgla_d = din("sgla", [NSEQ, 2, 256, 512])
    sret_d = din("sret", [NSEQ, 4, 128, 256])
    rope_d = din("rope", [4, NT, 128, 512])
    cst_d = din("cst", [128, 1024])
    cstb_d = din("cstb", [128, 512], BF16)

    y_d = dout("y", [1088, D])
    sgp_d = dout("sgp", [2, 256, 512])
    srp_d = dout("srp", [4, 128, 256])
    sgs_d = dout("sgs", [NSEQ, 2, 256, 512])
    srs_d = dout("srs", [NSEQ, 4, 128, 256])

    ib_d = [nc.dram_tensor("ib%d" % j, [256, NCOL], BF16).ap() for j in range(4)]
    ob_all = nc.dram_tensor("ob_all", [2048, NCOL], BF16).ap()
    ob_d = [ob_all[j * 512:(j + 1) * 512, :] for j in range(4)]
    pending_cc = []
    cc_toks = {}

    def flush_cc():
        while pending_cc:
            j, tok = pending_cc.pop(0)
            cc_toks[j] = B.op("pool", lambda e, j=j: e.collective_compute(
                "AllGather", ALU.bypass, replica_groups=[[0, 1], [2, 3], [4, 5], [6, 7]],
                ins=[ib_d[j].opt()], outs=[ob_d[j].opt()]), extra=[tok], special="cc%d" % j)

    es = ExitStack()
    arena = es.enter_context(nc.sbuf_tensor("arena", [128, 53000], F32))
    banks = [es.enter_context(nc.psum_tensor("pb%d" % i, [128, 512], F32)) for i in range(8)]
    bres = [Res(psum=True) for _ in range(8)]

    class Carver:
        def __init__(self, start=0):
            self.off = start

        def take(self, nbytes):
            nb = (nbytes + 63) // 64 * 64
            o = self.off
            self.off += nb
            assert self.off <= 53000 * 4, self.off
            return o

    def view(off, nelem, dt, parts=128):
        if dt == F32:
            return arena[0:parts, off // 4: off // 4 + nelem]
        nw = (nelem + 1) // 2
        return arena[0:parts, off // 4: off // 4 + nw].bitcast(BF16)[:, 0:nelem]

    class Tile:
        def __init__(self, car, nelem, dt, parts=128):
            self.off = car.take(nelem * (4 if dt == F32 else 2))
            self.n = nelem
            self.dt = dt
            self.ap = view(self.off, nelem, dt, parts)
            self.res = Res()

        def v3(self, a):
            return self.ap.rearrange("p (a b) -> p a b", a=a)

    car = Carver()
    cst = Tile(car, 1024, F32)
    cstb = Tile(car, 512, BF16)
    gw = Tile(car, 512, F32)
    rwt = Tile(car, 256, F32)
    rbt = Tile(car, 256, F32)
    wlr = Tile(car, 512, BF16, parts=32)
    wbuf = [Tile(car, 16 * WC, BF16) for _ in range(2)]
    wbres = [[Res() for _ in range(16)] for _ in range(2)]
    main_start = car.off

    UCP = cst.ap[:, 0:128]
    UCS = cst.ap[:, 128:256]
    ROWSEL = cst.ap[:, 256:288]
    SELNEG = cst.ap[:, 288:320]
    NEGCOL = cst.ap[:, 320:322]
    ONES = cst.ap[0:1, 384:512]
    RT = cst.ap[:, 512:544]
    C_LNQ = cst.ap[:, 544:545]
    C_HEPS = cst.ap[:, 545:546]
    C_LEPS = cst.ap[:, 546:547]
    C_ONE = cst.ap[:, 547:548]
    C_H0 = cst.ap[:, 548:549]
    C_H1 = cst.ap[:, 549:550]
    IDENT = cstb.ap[:, 0:128]
    MASKP = cstb.ap[:, 128:256]
    MASKS = cstb.ap[:, 256:384]

    car.off = main_start
    xst = [Tile(car, 2048, F32) for _ in range(1)]
    xtb = [Tile(car, 2048, BF16) for _ in range(3)]
    partA = Tile(car, NT * 512, F32)
    partA_res = [Res() for _ in range(NT)]
    mTb = Tile(car, 2 * NCOL, BF16)
    S = Tile(car, 1024, F32)
    Sbf = Tile(car, 1024, BF16)
    S0 = [Tile(car, 1024, F32) for _ in range(2)]
    S0bf = [Tile(car, 1024, BF16) for _ in range(2)]
    Sout = [Tile(car, 1024, F32) for _ in range(2)]
    qTs = Tile(car, 256, BF16)
    kts = Tile(car, 256, BF16)
    vs = Tile(car, 512, BF16)
    esm = Tile(car, 64, F32)
    oacc = Tile(car, 512, F32)
    qTm = [Tile(car, 256, BF16) for _ in range(2)]
    gs = Tile(car, 256, F32)
    ktm = [Tile(car, 256, BF16) for _ in range(2)]

    class WorkSet:
        pass

    wsets = []
    for _ in range(2):
        w = WorkSet()
        base = car.off
        w.lraT = Tile(car, 128, BF16, parts=32)
        w.esp = Tile(car, 512, F32)
        w.EE = Tile(car, 512, F32)

        class _A:
            pass
        def alias(t, a, b2):
            o = _A()
            o.ap = t.ap[:, a:b2]
            o.res = t.res
            return o
        w.ex = alias(w.esp, 0, 256)
        w.sp = alias(w.esp, 256, 512)
        w.Eb = alias(w.EE, 0, 256)
        w.Enb = alias(w.EE, 256, 512)
        w.sz = alias(w.esp, 0, 512)
        w.sg = alias(w.EE, 0, 512)
        w.on = alias(w.esp, 256, 512)
        w.rope = alias(w.EE, 256, 512)
        w.ecol = Tile(car, 16, F32)
        w.qk = Tile(car, 512, BF16)
        w.qtl = alias(w.qk, 0, 256)
        w.ktl = alias(w.qk, 256, 512)
        w.vbf = Tile(car, 512, BF16)
        w.qkT = Tile(car, 512, BF16)
        w.alias = alias
        w.scT = Tile(car, 128, BF16)
        w.st = Tile(car, 16, F32)
        w.mv = Tile(car, 8, F32)
        w.t1 = Tile(car, 256, F32)
        w.t2 = Tile(car, 256, F32)
        w.rope4 = Tile(car, 512, F32)
        w.mrg = Tile(car, 256, BF16)
        w.mT = Tile(car, 256, BF16)
        wsets.append(w)
    main_end = car.off

    car.off = main_start
    fA = [Tile(car, 2048, BF16) for _ in range(2)]
    fB = [Tile(car, 2048, BF16) for _ in range(2)]
    fT = [Tile(car, 2048, BF16) for _ in range(2)]
    fS = [Tile(car, 2048, BF16) for _ in range(3)]
    fx = [Tile(car, 2048, F32) for _ in range(3)]
    fy = [Tile(car, 2048, F32) for _ in range(3)]
    fn_ = [Tile(car, 2048, F32) for _ in range(2)]
    lnw = Tile(car, 2048, F32)
    lnb = Tile(car, 2048, F32)
    fst = [Tile(car, 32, F32) for _ in range(3)]
    fmv = [Tile(car, 8, F32) for _ in range(3)]

    def _fsize(ap):
        n = 1
        for d in ap.shape[1:]:
            n *= d
        return n

    def mm(out, lhsT, rhs, start=True, stop=True):
        f = lambda e: e.matmul(out, lhsT, rhs, start=start, stop=stop)
        c = max(64.0, 12.0 + _fsize(out) / 2.4)
        if lhsT.dtype == F32:
            c *= 4
        f.cost = c
        return f

    def tr(out, in_):
        f = lambda e: e.transpose(out, in_, IDENT)
        f.cost = 120.0
        return f

    def seq(*fns):
        def f(e):
            r = None
            for g in fns:
                r = g(e)
            return r
        f.cost = sum(getattr(g, "cost", 150.0) for g in fns)
        return f

    def act(out, in_, func, bias=None, scale=None):
        def f(e):
            kw = {}
            if bias is not None:
                kw["bias"] = bias
            if scale is not None:
                kw["scale"] = scale
            return e.activation(out, in_, func, **kw)
        f.cost = 230.0 + 0.9 * _fsize(out)
        return f

    def tt(out, a, b, op):
        f = lambda e: e.tensor_tensor(out, a, b, op)
        f.cost = 150.0 + 1.05 * _fsize(out)
        f.pcost = 250.0 + 2.1 * _fsize(out)
        return f

    def ts(out, a, s1, s2, op0, op1=None):
        if op1 is None:
            f = lambda e: e.tensor_scalar(out, a, s1, None, op0)
        else:
            f = lambda e: e.tensor_scalar(out, a, s1, s2, op0, op1)
        f.cost = 150.0 + 1.05 * _fsize(out)
        return f

    def stt(out, a, s_, b, op0, op1):
        f = lambda e: e.scalar_tensor_tensor(out, a, s_, b, op0, op1)
        f.cost = 150.0 + 1.05 * _fsize(out)
        return f

    def cp(out, in_):
        f = lambda e: e.tensor_copy(out, in_)
        f.cost = 150.0 + 1.0 * _fsize(out)
        f.pcost = 250.0 + 2.0 * _fsize(out)
        return f

    def dmaf(out, in_):
        f = lambda e: e.dma_start(out=out, in_=in_)
        n = 1
        for d in out.shape:
            n *= d
        f.nbytes = n * (4 if (out.dtype == F32 or in_.dtype == F32) else 2)
        return f

    B.dma("c0", dmaf(cst.ap, cst_d), writes=[cst.res])
    B.dma("c1", dmaf(cstb.ap, cstb_d), writes=[cstb.res])
    B.dma("c2", dmaf(gw.ap, gw_d.partition_broadcast(128)), writes=[gw.res])
    B.dma("c3", dmaf(wlr.ap[0:16, :], wlr_d), writes=[wlr.res], q="pool")
    B.dma("c4", dmaf(wlr.ap[16:17, :], blr_d), writes=[wlr.res], q="pool")

    wload_state = {"n": 0}
    last_x = {"op": None}

    def load_unit_ktile(u, kt):
        n = wload_state["n"]
        wload_state["n"] += 1
        dst = wbuf[u % 2].v3(16)[:, kt, :]
        ex_ = [last_x["op"]] if (last_x["op"] is not None and u < 9) else []
        B.dma("wl%d_%d" % (u % 2, kt), dmaf(dst, wu_d[u, kt * 128:(kt + 1) * 128, :]), writes=[wbres[u % 2][kt]], q="pool", extra=ex_)

    for kt in range(16):
        load_unit_ktile(0, kt)

    def proj(u, xt, cols, bank, c0=0):
        wb = wbuf[u % 2].v3(16)
        n = cols[1] - cols[0]
        step = 16 // PSPLIT
        for k0 in range(0, 16, step):
            fns = []
            for kt in range(k0, k0 + step):
                fns.append(mm(banks[bank][:, c0:c0 + n], xt.v3(16)[:, kt, :], wb[:, kt, cols[0]:cols[1]],
                              start=(kt == 0), stop=(kt == 15)))
            B.op("pe", seq(*fns), reads=[xt.res] + wbres[u % 2][k0:k0 + step], writes=[bres[bank]])

    xload_n = {"n": 0}

    def load_x(i):
        n = xload_n["n"]
        xload_n["n"] += 1
        xt = xtb[n % 3]
        last_x["op"] = B.dma("xs%d" % (n % 3), dmaf(xt.ap, xT_d[i]), writes=[xt.res], q="pool")
        return xt

    tile_order = [16] + list(range(16))

    def store_state_prompt(dst_ap, nk, dv):
        B.dma("sp_st", dmaf(dst_ap.rearrange("(k p) v -> p k v", p=128), S.ap[:, 0:nk * dv].rearrange("p (k v) -> p k v", k=nk)),
              reads=[S.res])

    def step_scores(w, i, nk):
        T = 6
        fns = []
        for kt in range(nk):
            fns.append(mm(banks[T][:, 256:384], w.kT.ap[:, kt * 128:(kt + 1) * 128], w.qT.ap[:, kt * 128:(kt + 1) * 128],
                          start=(kt == 0), stop=(kt == nk - 1)))
        B.op("pe", seq(*fns), reads=[w.kT.res, w.qT.res], writes=[bres[T]])

    def step_mask(w, i):
        T = 6
        mask = MASKS if i == 16 else MASKP
        B.op("dve", tt(w.scT.ap, banks[T][:, 256:384], mask, ALU.mult), reads=[bres[T], cstb.res], writes=[w.scT.res])

    def step_oP(w, i, nk, dv, pbank):
        is_s = (i == 16)
        O = 7
        fns = [mm(banks[O][:, 0:dv], w.scT.ap, w.vbf.ap[:, 0:dv], start=True, stop=is_s)]
        rd = [w.scT.res, w.vbf.res]
        if not is_s:
            for kt in range(nk):
                fns.append(mm(banks[O][:, 0:dv], w.qT.ap[:, kt * 128:(kt + 1) * 128], Sbf.ap[:, kt * dv:(kt + 1) * dv],
                              start=False, stop=(kt == nk - 1)))
            rd += [w.qT.res, Sbf.res]
        B.op("pe", seq(*fns), reads=rd, writes=[bres[O]])
        if is_s:
            B.op("act", act(oacc.ap[:, 0:dv], banks[O][:, 0:dv], AF.Copy), reads=[bres[O]], writes=[oacc.res])
            return
        for kt in range(nk):
            pb = pbank[kt]
            B.op("pe", mm(banks[pb][:, 0:dv], w.ktl.ap[:, kt * 128:(kt + 1) * 128], w.vbf.ap[:, 0:dv]),
                 reads=[w.ktl.res, w.vbf.res], writes=[bres[pb]])
            sl = S.ap[:, kt * dv:(kt + 1) * dv]
            B.op("dve", tt(sl, sl, banks[pb][:, 0:dv], ALU.add), reads=[bres[pb]], writes=[S.res])

    def step_scale_state(nk, dv, e_ap):
        for kt in range(nk):
            sl = S.ap[:, kt * dv:(kt + 1) * dv]
            B.op("act", act(Sbf.ap[:, kt * dv:(kt + 1) * dv], sl, AF.Copy, scale=e_ap[kt]), reads=[S.res] + e_ap[2], writes=[Sbf.res])
            B.op("act", act(sl, sl, AF.Copy, scale=e_ap[kt]), reads=[S.res] + e_ap[2], writes=[S.res])

    S0sub = {nm: [Res() for _ in range(8)] for nm in ("s0", "s0bf", "sout")}

    class _SB:
        pass

    def seq_bufs(sq, nk, dv, slot):
        b = _SB()
        if isinstance(slot, tuple):
            ss = slot[1]
            r, sub = ss // 4, ss % 4
            sl = slice(sub * 256, sub * 256 + nk * dv)
            b.s0, b.s0r = S0[r].ap[:, sl], [S0sub["s0"][ss]]
            b.s0bf, b.s0bfr = S0bf[r].ap[:, sl], [S0sub["s0bf"][ss]]
            b.sout, b.soutr = Sout[r].ap[:, sl], [S0sub["sout"][ss]]
            b.key = "r%d" % ss
        else:
            r = sq % 2 if slot is None else slot
            b.s0, b.s0r = S0[r].ap[:, 0:nk * dv], S0sub["s0"][4 * r:4 * r + 4]
            b.s0bf, b.s0bfr = S0bf[r].ap[:, 0:nk * dv], S0sub["s0bf"][4 * r:4 * r + 4]
            b.sout, b.soutr = Sout[r].ap[:, 0:nk * dv], S0sub["sout"][4 * r:4 * r + 4]
            b.key = "%d" % r
        return b

    def sample_seq_load(sq, nk, dv, state_d, slot=None):
        sb = seq_bufs(sq, nk, dv, slot)
        B.dma("s0_" + sb.key, dmaf(sb.s0.rearrange("p (k v) -> p k v", k=nk),
                                   state_d.rearrange("(k p) v -> p k v", p=128)), writes=sb.s0r)
        for kt in range(nk):
            sl = slice(kt * dv, (kt + 1) * dv)
            if nk == 1 and sq % 2 == 1:
                B.op("dve", cp(sb.s0bf[:, sl], sb.s0[:, sl]), reads=sb.s0r, writes=sb.s0bfr)
            elif kt == 0:
                B.op("act", act(sb.s0bf[:, sl], sb.s0[:, sl], AF.Copy), reads=sb.s0r, writes=sb.s0bfr)
            else:
                B.op("dve", cp(sb.s0bf[:, sl], sb.s0[:, sl]), reads=sb.s0r, writes=sb.s0bfr)

    def sample_seq_ops(sq, nk, dv, out_d, e_fn, obank, pbanks, mask_eng="dve", slot=None):
        r = sq % 2
        sb = seq_bufs(sq, nk, dv, slot)
        qv = qTm[r].ap[:, 0:nk * 128].rearrange("p (k t) -> p k t", k=nk)
        qs = qTs.ap[:, 0:nk * 128].rearrange("p (k t) -> p k t", k=nk)
        B.op("pool", cp(qv[:, :, 4 * sq:4 * sq + 4], qs[:, :, 4 * sq:4 * sq + 4]), reads=[qTs.res], writes=[qTm[r].res])
        if mask_eng == "act":
            B.op("act", act(ktm[r].ap[:, 0:nk * 128], kts.ap[:, 0:nk * 128], AF.Copy, scale=ROWSEL[:, sq:sq + 1]),
                 reads=[kts.res, cst.res], writes=[ktm[r].res])
        else:
            B.op("dve", ts(ktm[r].ap[:, 0:nk * 128], kts.ap[:, 0:nk * 128], ROWSEL[:, sq:sq + 1], None, ALU.mult),
                 reads=[kts.res, cst.res], writes=[ktm[r].res])
        fns = []
        for kt in range(nk):
            fns.append(mm(banks[obank][:, 0:dv], qv[:, kt, :], sb.s0bf[:, kt * dv:(kt + 1) * dv], start=(kt == 0), stop=(kt == nk - 1)))
        B.op("pe", seq(*fns), reads=[qTm[r].res] + sb.s0bfr, writes=[bres[obank]])
        B.op("dve", tt(oacc.ap[:, 0:dv], oacc.ap[:, 0:dv], banks[obank][:, 0:dv], ALU.add), reads=[bres[obank]], writes=[oacc.res])
        for kt in range(nk):
            pb = pbanks[kt]
            B.op("pe", mm(banks[pb][:, 0:dv], ktm[r].ap[:, kt * 128:(kt + 1) * 128], vs.ap[:, 0:dv]),
                 reads=[ktm[r].res, vs.res], writes=[bres[pb]])
        B.op("pool", lambda e: e.memset(qv[:, :, 4 * sq:4 * sq + 4], 0.0), writes=[qTm[r].res])
        for kt in range(nk):
            pb = pbanks[kt]
            sl = sb.s0[:, kt * dv:(kt + 1) * dv]
            B.op("dve", tt(sl, sl, banks[pb][:, 0:dv], ALU.add), reads=[bres[pb]], writes=sb.s0r)
        e_ap = e_fn(sq)
        for kt in range(nk):
            B.op("act", act(sb.sout[:, kt * dv:(kt + 1) * dv], sb.s0[:, kt * dv:(kt + 1) * dv], AF.Copy, scale=e_ap[kt]),
                 reads=sb.s0r + e_ap[2], writes=sb.soutr)
        B.dma("so_" + sb.key, dmaf(out_d.rearrange("(k p) v -> p k v", p=128),
                                   sb.sout.rearrange("p (k v) -> p k v", k=nk)), reads=sb.soutr)

    def rstd_from(var_ap, eps_ap, out_ap, w, scale=1.0):
        B.op("act", act(out_ap, var_ap, AF.Ln, bias=eps_ap, scale=scale), reads=[w.mv.res, cst.res], writes=[w.mv.res])
        B.op("act", act(out_ap, out_ap, AF.Exp, scale=-0.5), reads=[w.mv.res], writes=[w.mv.res])

    for w_ in wsets:
        B.op("pool", lambda e, w_=w_: e.memset(w_.lraT.ap, 1.0), writes=[w_.lraT.res])
    B.op("pool", lambda e: e.memset(qTm[0].ap, 0.0), writes=[qTm[0].res])
    B.op("pool", lambda e: e.memset(qTm[1].ap, 0.0), writes=[qTm[1].res])

    def bankset(n):
        return (0, 1) if n % 2 == 0 else (2, 3)

    def pass_arec(u, h):
        nk, dv = 2, 512
        B.op("pool", lambda e: e.memset(S.ap, 0.0), writes=[S.res])
        B.op("pool", lambda e: e.memset(Sbf.ap, 0.0), writes=[Sbf.res])
        G, M, T, O = 4, 5, 6, 7
        xts = {}
        wb = wbuf[u % 2].v3(16)
        for w_ in wsets:
            w_.qT = w_.alias(w_.qkT, 0, 256)
            w_.kT = w_.alias(w_.qkT, 256, 512)
            w_.qtl = w_.alias(w_.qk, 0, 256)
            w_.ktl = w_.alias(w_.qk, 256, 512)

        def L(n):
            xt = xts[n]
            fns = [mm(banks[M][0:16, 0:128], wb[:, kt, 1024:1040], xt.v3(16)[:, kt, :], start=(kt == 0), stop=(kt == 15))
                   for kt in range(16)]
            B.op("pe", seq(*fns), reads=[xt.res] + wbres[u % 2], writes=[bres[M]])

        def A0(n):
            proj(u, xts[n], (0, 512), bankset(n)[0])

        def A1(n):
            proj(u, xts[n], (512, 1024), bankset(n)[1])

        def B1(n):
            w = wsets[n % 2]
            B.op("act", act(w.lraT.ap[0:16, :], banks[M][0:16, 0:128], AF.Copy), reads=[bres[M]], writes=[w.lraT.res])
            B.op("pe", mm(banks[G][:, 0:256], w.lraT.ap[0:17, :], wlr.ap[0:17, h * 256:(h + 1) * 256]),
                 reads=[w.lraT.res, wlr.res], writes=[bres[G]])

        def B2(n):
            w = wsets[n % 2]
            B.op("act", act(w.ex.ap, banks[G][:, 0:256], AF.Exp, scale=-1.0), reads=[bres[G]], writes=[w.ex.res])
            B.op("act", act(w.sp.ap, w.ex.ap, AF.Ln, bias=C_ONE), reads=[w.ex.res, cst.res], writes=[w.sp.res])

        def B3(n):
            w = wsets[n % 2]
            is_s = (tile_order[n] == 16)
            uc = UCS if is_s else UCP
            B.op("pe", mm(banks[G][:, 256:512], uc, w.sp.ap), reads=[w.sp.res, cst.res], writes=[bres[G]])
            if is_s:
                fns = [mm(banks[M][:, 256 + 32 * kt:256 + 32 * kt + 32], w.sp.ap[:, kt * 128:(kt + 1) * 128], SELNEG) for kt in range(2)]
            else:
                fns = [mm(banks[M][:, 256 + 2 * kt:256 + 2 * kt + 2], w.sp.ap[:, kt * 128:(kt + 1) * 128], NEGCOL) for kt in range(2)]
            B.op("pe", seq(*fns), reads=[w.sp.res, cst.res], writes=[bres[M]])

        def B4(n):
            w = wsets[n % 2]
            is_s = (tile_order[n] == 16)
            pa, pv = bankset(n)
            B.op("act", act(w.Eb.ap, banks[G][:, 256:512], AF.Exp, bias=C_LNQ), reads=[bres[G], cst.res], writes=[w.Eb.res])
            B.op("act", act(w.Enb.ap, banks[G][:, 256:512], AF.Exp, scale=-1.0), reads=[bres[G]], writes=[w.Enb.res])
            B.op("dve", tt(w.qtl.ap, banks[pa][:, 0:256], w.Eb.ap, ALU.mult), reads=[bres[pa], w.Eb.res], writes=[w.qtl.res])
            B.op("dve", tt(w.ktl.ap, banks[pa][:, 256:512], w.Enb.ap, ALU.mult), reads=[bres[pa], w.Enb.res], writes=[w.ktl.res])
            if is_s:
                B.op("act", act(esm.ap, banks[M][:, 256:320], AF.Exp), reads=[bres[M]], writes=[esm.res])
            else:
                B.op("act", act(w.ecol.ap[:, 0:4], banks[M][:, 256:260], AF.Exp), reads=[bres[M]], writes=[w.ecol.res])
            B.op("act", act(w.vbf.ap, banks[pv][:, 0:512], AF.Copy), reads=[bres[pv]], writes=[w.vbf.res])

        def Tr(n):
            w = wsets[n % 2]
            tb = banks[T][:, 0:256].bitcast(BF16)
            fns = []
            for kt in range(2):
                fns.append(lambda e, kt=kt, qa=w.qtl.ap: e.transpose(tb[:, kt * 128:(kt + 1) * 128], qa[:, kt * 128:(kt + 1) * 128], IDENT))
                fns.append(lambda e, kt=kt, ka=w.ktl.ap: e.transpose(tb[:, 256 + kt * 128:256 + (kt + 1) * 128], ka[:, kt * 128:(kt + 1) * 128], IDENT))
            B.op("pe", seq(*fns), reads=[w.qtl.res, w.ktl.res, cstb.res], writes=[bres[T]])

        def cpT(n):
            w = wsets[n % 2]
            tb = banks[T][:, 0:256].bitcast(BF16)
            B.op("dve", cp(w.qkT.ap, tb[:, 0:512]), reads=[bres[T]], writes=[w.qkT.res])
            if tile_order[n] == 16:
                B.op("pool", cp(qTs.ap, w.qT.ap), reads=[w.qT.res], writes=[qTs.res])
                B.op("pool", cp(kts.ap, w.ktl.ap), reads=[w.ktl.res], writes=[kts.res])
                B.op("pool", cp(vs.ap, w.vbf.ap), reads=[w.vbf.res], writes=[vs.res])

        def Dn(n):
            w = wsets[n % 2]
            i = tile_order[n]
            if i == 16:
                return
            step_scale_state(nk, dv, (w.ecol.ap[:, 0:1], w.ecol.ap[:, 2:3], [w.ecol.res]))
            finish_o_gla(w, i, banks[O][:, 0:512], bres[O])

        def D2b(n, fb):
            i = tile_order[n]
            if i == 16:
                return
            ef = lambda s_: (esm.ap[:, s_:s_ + 1], esm.ap[:, 32 + s_:33 + s_], [esm.res])
            sample_seq_ops(2 * i, nk, dv, sgs_d[2 * i, h], ef, G, (T, M), slot=i % 2)

        def seq_loads(n):
            i = tile_order[n]
            if i == 16:
                return
            sample_seq_load(2 * i, nk, dv, sgla_d[2 * i, h], slot=i % 2)

        xts[0] = load_x(tile_order[0])
        xts[1] = load_x(tile_order[1])
        L(0); A1(0); B1(0); A0(0); B2(0); B3(0); B4(0)
        for n in range(NT):
            w = wsets[n % 2]
            i = tile_order[n]
            nx = n + 1 < NT
            if n + 2 < NT:
                xts[n + 2] = load_x(tile_order[n + 2])
            if n < 16:
                load_unit_ktile(u + 1, n)
            if n == 2:
                flush_cc()
            Tr(n)
            if nx:
                L(n + 1); A1(n + 1); B1(n + 1)
            cpT(n)
            step_scores(w, i, nk)
            if nx:
                A0(n + 1); B2(n + 1)
            step_mask(w, i)
            if nx:
                B3(n + 1)
            step_oP(w, i, nk, dv, bankset(n))
            if nx:
                B4(n + 1)
            Dn(n)
            if n >= 1:
                D2b(n - 1, bankset(n + 1))
            seq_loads(n)
        D2b(NT - 1, bankset(NT))
        store_state_prompt(sgp_d[h], 2, 512)

    def finish_o_gla(w, i, o_ap, o_res):
        B.op("dve", lambda e: e.bn_stats(w.st.ap[:, 0:6], o_ap), reads=[o_res], writes=[w.st.res])
        B.op("dve", lambda e: e.bn_aggr(w.mv.ap[:, 0:2], w.st.ap[:, 0:6]), reads=[w.st.res], writes=[w.mv.res])
        B.op("dve", stt(w.mv.ap[:, 2:3], w.mv.ap[:, 0:1], w.mv.ap[:, 0:1], w.mv.ap[:, 1:2], ALU.mult, ALU.add),
             reads=[w.mv.res], writes=[w.mv.res])
        rstd_from(w.mv.ap[:, 2:3], C_HEPS, w.mv.ap[:, 3:4], w)
        dst = partA.ap[:, i * 512:(i + 1) * 512]
        B.op("dve", stt(dst, o_ap, w.mv.ap[:, 3:4], gw.ap, ALU.mult, ALU.mult), reads=[o_res, w.mv.res, gw.res], writes=[partA_res[i]])

    def pass_agate(u, h):
        order_g = list(range(16)) + [16]
        ef = lambda s_: (esm.ap[:, s_:s_ + 1], esm.ap[:, 32 + s_:33 + s_], [esm.res])
        xts = {}
        xts[0] = load_x(order_g[0])
        xts[1] = load_x(order_g[1])
        for n, i in enumerate(order_g):
            w = wsets[n % 2]
            pa, pv = 2 * (n % 3), 2 * (n % 3) + 1
            xt = xts[n]
            if n + 2 < NT:
                xts[n + 2] = load_x(order_g[n + 2])
            if n < 16:
                load_unit_ktile(u + 1, n)
            if n == 2:
                flush_cc()
            if n == 0:
                sample_seq_load(1, 2, 512, sgla_d[1, h], slot=0)
            if n + 1 < 16:
                sample_seq_load(2 * (n + 1) + 1, 2, 512, sgla_d[2 * (n + 1) + 1, h], slot=(n + 1) % 2)
            if i == 16:
                finish_o_gla(wsets[0], 16, oacc.ap, oacc.res)
            proj(u, xt, (0, 512), pa)
            proj(u, xt, (512, 1024), pv)
            B.op("act", act(w.sz.ap, banks[pa][:, 0:512], AF.Exp, scale=-1.0), reads=[bres[pa]], writes=[w.sz.res])
            B.op("act", act(w.sz.ap, w.sz.ap, AF.Ln, bias=C_ONE), reads=[cst.res], writes=[w.sz.res])
            B.op("act", act(w.sg.ap, banks[pv][:, 0:512], AF.Exp, scale=-1.0), reads=[bres[pv]], writes=[w.sg.res])
            B.op("act", act(w.sg.ap, w.sg.ap, AF.Ln, bias=C_ONE), reads=[cst.res], writes=[w.sg.res])
            B.op("pool", tt(w.sz.ap, w.sz.ap, w.sg.ap, ALU.add), reads=[w.sg.res], writes=[w.sz.res])
            B.op("act", act(w.sz.ap, w.sz.ap, AF.Exp, scale=-1.0), reads=[], writes=[w.sz.res])
            B.op("dve", tt(w.sz.ap, banks[pa][:, 0:512], w.sz.ap, ALU.mult), reads=[bres[pa]], writes=[w.sz.res])
            dst = partA.ap[:, i * 512:(i + 1) * 512]
            B.op("pool", tt(dst, dst, w.sz.ap, ALU.mult), reads=[w.sz.res], writes=[partA_res[i]])
            if i != 16:
                sample_seq_ops(2 * i + 1, 2, 512, sgs_d[2 * i + 1, h], ef, 6, (7, 6), slot=i % 2)

    def tile_cols(i):
        if i < 16:
            return [(0, 128, (i // 8) * 1088 + (i % 8) * 128)]
        return [(0, 64, 1024), (64, 64, 1088 + 1024)]

    def pass_ret(u, j):
        nk, dv = 1, 256
        h = j // 2
        pcol = (j % 2) * 256
        finish_o_ret.pcol = pcol
        B.op("pool", lambda e: e.memset(S.ap, 0.0), writes=[S.res])
        B.op("pool", lambda e: e.memset(Sbf.ap, 0.0), writes=[Sbf.res])
        B.dma("c5", dmaf(rwt.ap, rw_d[:, j * 256:(j + 1) * 256].partition_broadcast(128)), writes=[rwt.res])
        B.dma("c6", dmaf(rbt.ap, rb_d[:, j * 256:(j + 1) * 256].partition_broadcast(128)), writes=[rbt.res])
        T, O = 6, 7
        rt = RT[:, 8 * j:8 * j + 8]
        xts = {}
        for w_ in wsets:
            w_.qT = w_.alias(w_.qkT, 0, 128)
            w_.kT = w_.alias(w_.qkT, 128, 256)
            w_.qtl = w_.alias(w_.qk, 0, 128)
            w_.ktl = w_.alias(w_.qk, 128, 256)

        def A0(n):
            proj(u, xts[n], (0, 512), bankset(n)[0])
            wv = wsets[n % 2]
            B.dma("rp%d" % (n % 2), dmaf(wv.rope4.ap, rope_d[j, tile_order[n]]), writes=[wv.rope4.res])

        def A1(n):
            proj(u, xts[n], (512, 1024), bankset(n)[1])

        def Bq(n):
            w = wsets[n % 2]
            is_s = (tile_order[n] == 16)
            pa = bankset(n)[0]
            C4, S4 = w.rope4.ap[:, 0:256], w.rope4.ap[:, 256:512]
            X = banks[pa][:, 0:256]
            X4 = X.rearrange("p (a b c) -> p a b c", a=2, b=2)
            S44 = S4.rearrange("p (a b c) -> p a b c", a=2, b=2)
            t24 = w.t2.ap.rearrange("p (a b c) -> p a b c", a=2, b=2)
            B.op("dve", tt(w.t1.ap, X, C4, ALU.mult), reads=[bres[pa], w.rope4.res], writes=[w.t1.res])
            B.op("dve", tt(t24[:, :, 0, :], X4[:, :, 1, :], S44[:, :, 0, :], ALU.mult), reads=[bres[pa], w.rope4.res], writes=[w.t2.res])
            B.op("dve", tt(t24[:, :, 1, :], X4[:, :, 0, :], S44[:, :, 1, :], ALU.mult), reads=[bres[pa], w.rope4.res], writes=[w.t2.res])
            B.op("pool", tt(w.qk.ap[:, 0:256], w.t1.ap, w.t2.ap, ALU.add), reads=[w.t1.res, w.t2.res], writes=[w.qk.res])
            B.op("act", act(w.vbf.ap[:, 0:256], banks[pa][:, 256:512], AF.Copy), reads=[bres[pa]], writes=[w.vbf.res])

        def Tr(n):
            w = wsets[n % 2]
            tb = banks[T][:, 0:256].bitcast(BF16)
            B.op("pe", seq(lambda e, qa=w.qtl.ap: e.transpose(tb[:, 0:128], qa[:, 0:128], IDENT),
                           lambda e, ka=w.ktl.ap: e.transpose(tb[:, 128:256], ka[:, 0:128], IDENT)),
                 reads=[w.qtl.res, w.ktl.res, cstb.res], writes=[bres[T]])

        def cpT(n):
            w = wsets[n % 2]
            tb = banks[T][:, 0:256].bitcast(BF16)
            B.op("dve", cp(w.qkT.ap[:, 0:256], tb[:, 0:256]), reads=[bres[T]], writes=[w.qkT.res])
            if tile_order[n] == 16:
                B.op("pool", cp(qTs.ap[:, 0:128], w.qT.ap[:, 0:128]), reads=[w.qT.res], writes=[qTs.res])
                B.op("pool", cp(kts.ap[:, 0:128], w.ktl.ap[:, 0:128]), reads=[w.ktl.res], writes=[kts.res])
                B.op("pool", cp(vs.ap[:, 0:256], w.vbf.ap[:, 0:256]), reads=[w.vbf.res], writes=[vs.res])

        def gates(n):
            w = wsets[n % 2]
            i = tile_order[n]
            pv = bankset(n)[1]
            B.op("act", act(w.sg.ap, banks[pv][:, 0:512], AF.Exp, scale=-1.0), reads=[bres[pv]], writes=[w.sg.res])
            B.op("act", act(w.sg.ap, w.sg.ap, AF.Ln, bias=C_ONE), reads=[cst.res], writes=[w.sg.res])
            B.op("act", act(w.sg.ap, w.sg.ap, AF.Exp, scale=-1.0), reads=[], writes=[w.sg.res])
            gdst = gs if i == 16 else w.sz
            B.op("dve", tt(w.sz.ap[:, 0:256], banks[pv][:, 0:256], w.sg.ap[:, 0:256], ALU.mult), reads=[bres[pv], w.sg.res], writes=[w.sz.res])
            B.op("dve", tt(gdst.ap[:, 0:256], w.sz.ap[:, 0:256], w.sg.ap[:, 256:512], ALU.mult), reads=[w.sz.res, w.sg.res], writes=[gdst.res])

        def D1(n):
            w = wsets[n % 2]
            i = tile_order[n]
            if i == 16:
                return
            step_scale_state(nk, dv, (rt[:, 4:5], None, [cst.res]))
            finish_o_ret(w, i, banks[O][:, 0:256], bres[O], w.sz)

        def D2a(n):
            i = tile_order[n]
            if i == 16:
                return
            merged_out(wsets[n % 2], i)

        def D2b(n, fb):
            i = tile_order[n]
            if i == 16:
                return
            ef = lambda s_: (rt[:, 5:6], None, [cst.res])
            sample_seq_ops(2 * i, nk, dv, srs_d[2 * i, j], ef, 5, (4,), mask_eng="act", slot=("r", (2 * i) % 8))
            sample_seq_ops(2 * i + 1, nk, dv, srs_d[2 * i + 1, j], ef, T, (fb[0],), mask_eng="act", slot=("r", (2 * i + 1) % 8))

        def seq_loads(n):
            i = tile_order[n]
            if i == 16:
                return
            for sq in (2 * i, 2 * i + 1):
                sample_seq_load(sq, nk, dv, sret_d[sq, j], slot=("r", sq % 8))

        xts[0] = load_x(tile_order[0])
        xts[1] = load_x(tile_order[1])
        for n_ in (1, 2):
            seq_loads(n_)
        A0(0); A1(0); Bq(0)
        for n in range(NT):
            w = wsets[n % 2]
            i = tile_order[n]
            nx = n + 1 < NT
            if n + 2 < NT:
                xts[n + 2] = load_x(tile_order[n + 2])
            if n < 16:
                load_unit_ktile(u + 1, n)
            if n == 2:
                flush_cc()
            Tr(n)
            gates(n)
            if nx:
                A0(n + 1)
            cpT(n)
            if nx:
                Bq(n + 1)
            if n >= 1:
                D2a(n - 1)
            step_scores(w, i, nk)
            if nx:
                A1(n + 1)
            step_mask(w, i)
            step_oP(w, i, nk, dv, bankset(n))
            D1(n)
            if n >= 1:
                D2b(n - 1, bankset(n + 1))
            if n + 3 < NT:
                seq_loads(n + 3)
        D2a(NT - 1)
        D2b(NT - 1, bankset(NT))
        w0 = wsets[0]
        finish_o_ret(w0, 16, oacc.ap[:, 0:256], oacc.res, gs)
        merged_out(w0, 16)
        store_state_prompt(srp_d[j], 1, 256)
        tok = B.dma("ibst", dmaf(ib_d[j].rearrange("(k p) c -> p k c", p=128), mTb.v3(2)), reads=[mTb.res])
        pending_cc.append((j, tok))

    def finish_o_ret(w, i, o_ap, o_res, g):
        pcol = finish_o_ret.pcol
        B.op("dve", lambda e: e.bn_stats(w.st.ap[:, 0:6], o_ap), reads=[o_res], writes=[w.st.res])
        B.op("dve", lambda e: e.bn_aggr(w.mv.ap[:, 0:2], w.st.ap[:, 0:6]), reads=[w.st.res], writes=[w.mv.res])
        rstd_from(w.mv.ap[:, 1:2], C_HEPS, w.mv.ap[:, 3:4], w)
        B.op("dve", stt(w.on.ap, o_ap, w.mv.ap[:, 0:1], rwt.ap, ALU.subtract, ALU.mult),
             reads=[o_res, w.mv.res, rwt.res], writes=[w.on.res])
        B.op("dve", stt(w.on.ap, w.on.ap, w.mv.ap[:, 3:4], rbt.ap, ALU.mult, ALU.add),
             reads=[w.mv.res, rbt.res], writes=[w.on.res])
        B.op("dve", tt(w.on.ap, w.on.ap, g.ap[:, 0:256], ALU.mult), reads=[g.res], writes=[w.on.res])
        pa_sl = partA.ap[:, i * 512 + pcol:i * 512 + pcol + 256]
        B.op("dve", tt(w.mrg.ap, w.on.ap, pa_sl, ALU.add), reads=[w.on.res, partA_res[i]], writes=[w.mrg.res])

    def merged_out(w, i):
        T = 4
        tb = banks[T][:, 0:256].bitcast(BF16)
        B.op("pe", seq(lambda e, ma=w.mrg.ap: e.transpose(tb[:, 0:128], ma[:, 0:128], IDENT),
                       lambda e, ma=w.mrg.ap: e.transpose(tb[:, 128:256], ma[:, 128:256], IDENT)),
             reads=[w.mrg.res, cstb.res], writes=[bres[T]])
        mv3 = mTb.v3(2)
        tb3 = tb[:, 0:256].rearrange("p (k t) -> p k t", k=2)
        for (t0, nt_, c0) in tile_cols(i):
            B.op("act", act(mv3[:, :, c0:c0 + nt_], tb3[:, :, t0:t0 + nt_], AF.Copy), reads=[bres[T]], writes=[mTb.res])

    import os
    npass = int(os.environ.get("K_NPASS", "8"))
    for u in range(npass):
        kind = KINDS[u]
        if kind == "arec":
            pass_arec(u, u // 4)
        elif kind == "agate":
            pass_agate(u, u // 4)
        else:
            pass_ret(u, (u // 4) * 2 + (u % 4 - 2))

    kfinal = int(os.environ.get("K_FINAL", "2"))
    sinks0 = B.sinks()
    for kt in range(16 if kfinal >= 1 else 0):
        load_unit_ktile(9, kt)
    flush_cc()
    bar_t = Tile(Carver(main_end), 16, F32)
    bar = B.op("pool", lambda e: e.memset(bar_t.ap, 0.0), extra=sinks0)
    B.extra = [bar]
    B.dma("c7", dmaf(lnw.ap, lnw_d.partition_broadcast(128)), writes=[lnw.res])
    B.dma("c8", dmaf(lnb.ap, lnb_d.partition_broadcast(128)), writes=[lnb.res])
    KSPL = ((0, 12), (12, 16))
    fres = {nm: [[Res(), Res()] for _ in range(3)] for nm in ("A", "B", "T", "S")}

    def floads(m):
        nt_ = 128 if m < 8 else 64
        r = m % 2
        a3, b3 = fA[r].v3(16), fB[r].v3(16)
        for part, (k0, k1) in enumerate(KSPL):
            ex_ = [cc_toks[j] for j in sorted(cc_toks) if (j < 3) == (part == 0)]
            rows = slice(k0 * 128, k1 * 128)
            B.dma("fa%d_%d" % (r, part), dmaf(a3[:, k0:k1, 0:nt_], ob_all[rows, m * 128:m * 128 + nt_].rearrange("(k p) c -> p k c", p=128)),
                  writes=[fres["A"][r][part]], extra=ex_)
            B.dma("fb%d_%d" % (r, part), dmaf(b3[:, k0:k1, 0:nt_], ob_all[rows, 1088 + m * 128:1088 + m * 128 + nt_].rearrange("(k p) c -> p k c", p=128)),
                  writes=[fres["B"][r][part]], extra=ex_)
        q = m % 3
        B.dma("fx%d" % q, dmaf(fx[q].ap[0:nt_, :], xres_d[m * 128:m * 128 + nt_, :]), writes=[fx[q].res])

    def fselect(m):
        nt_ = 128 if m < 8 else 64
        r = m % 2
        q = m % 3
        a3, b3, t3, s3 = fA[r].v3(16), fB[r].v3(16), fT[r].v3(16), fS[q].v3(16)
        for part, (k0, k1) in enumerate(KSPL):
            B.op("act", act(t3[:, k0:k1, 0:nt_], a3[:, k0:k1, 0:nt_], AF.Copy, scale=C_H0), reads=[fres["A"][r][part], cst.res], writes=[fres["T"][r][part]])
            B.op("dve", stt(s3[:, k0:k1, 0:nt_], b3[:, k0:k1, 0:nt_], C_H1, t3[:, k0:k1, 0:nt_], ALU.mult, ALU.add),
                 reads=[fres["B"][r][part], fres["T"][r][part], cst.res], writes=[fres["S"][q][part]])

    if kfinal >= 2:
        floads(0)
    for m in range(9 if kfinal >= 2 else 0):
        nt_ = 128 if m < 8 else 64
        r = m % 2
        q = m % 3
        s3 = fS[q].v3(16)
        if m + 1 < 9:
            floads(m + 1)
        if m == 0:
            fselect(0)
        if m + 1 < 9:
            fselect(m + 1)
        for blk in range(4):
            bk = 4 * r + blk
            wb = wbuf[blk // 2].v3(16)
            for part, (k0, k1) in enumerate(KSPL):
                fns = [mm(banks[bk][0:nt_, 0:512], s3[:, kt, 0:nt_], wb[:, kt, (blk % 2) * 512:(blk % 2) * 512 + 512],
                          start=(kt == 0), stop=(kt == 15)) for kt in range(k0, k1)]
                B.op("pe", seq(*fns), reads=[fres["S"][q][part]] + wbres[blk // 2][k0:k1], writes=[bres[bk]])
        for blk in range(4):
            bk = 4 * r + blk
            ysl = fy[q].ap[0:nt_, blk * 512:(blk + 1) * 512]
            B.op("dve", stt(ysl, fx[q].ap[0:nt_, blk * 512:(blk + 1) * 512], ALPHA, banks[bk][0:nt_, 0:512], ALU.mult, ALU.add),
                 reads=[fx[q].res, bres[bk]], writes=[fy[q].res])
            B.op("dve", lambda e, ysl=ysl, blk=blk, q=q, nt_=nt_: e.bn_stats(fst[q].ap[0:nt_, blk * 6:(blk + 1) * 6], ysl),
                 reads=[fy[q].res], writes=[fst[q].res])
        mv = fmv[q].ap
        B.op("dve", lambda e, q=q, nt_=nt_: e.bn_aggr(fmv[q].ap[0:nt_, 0:2], fst[q].ap[0:nt_, 0:24]), reads=[fst[q].res], writes=[fmv[q].res])
        B.op("act", act(mv[0:nt_, 3:4], mv[0:nt_, 1:2], AF.Ln, bias=C_LEPS[0:nt_, :]), reads=[fmv[q].res, cst.res], writes=[fmv[q].res])
        B.op("act", act(mv[0:nt_, 3:4], mv[0:nt_, 3:4], AF.Exp, scale=-0.5), reads=[fmv[q].res], writes=[fmv[q].res])
        B.op("dve", stt(mv[0:nt_, 4:5], mv[0:nt_, 0:1], -1.0, mv[0:nt_, 3:4], ALU.mult, ALU.mult), reads=[fmv[q].res], writes=[fmv[q].res])
        B.op("act", act(fn_[r].ap[0:nt_, :], fy[q].ap[0:nt_, :], AF.Identity, bias=mv[0:nt_, 4:5], scale=mv[0:nt_, 3:4]),
             reads=[fy[q].res, fmv[q].res], writes=[fn_[r].res])
        B.op("dve", tt(fn_[r].ap[0:nt_, :], fn_[r].ap[0:nt_, :], lnw.ap[0:nt_, :], ALU.mult), reads=[lnw.res], writes=[fn_[r].res])
        B.op("pool", tt(fn_[r].ap[0:nt_, :], fn_[r].ap[0:nt_, :], lnb.ap[0:nt_, :], ALU.add), reads=[lnb.res], writes=[fn_[r].res])
        B.dma("yo%d" % r, dmaf(y_d[m * 128:m * 128 + nt_, :], fn_[r].ap[0:nt_, :]), reads=[fn_[r].res])

    order = B.schedule()
    if os.environ.get('K_VERBOSE'):
        print('estimated makespan us', B.makespan / 1000.0, {k: len(v) for k, v in order.items()}, {k: round(sum(o.issue for o in v) / 1000.0) for k, v in order.items()})
    sems = {}
    for e in ("pe", "act", "dve", "pool"):
        sems[e] = es.enter_context(nc.semaphore("s_" + e))
    cnt = {}
    for st in Builder.STREAMS:
        for o in order[st]:
            if o.kind == "eng":
                cnt[st] = cnt.get(st, 0) + 1
                o.tok = (st, cnt[st])
            elif o.kind == "dma":
                k = "d:" + o.key
                if k not in sems:
                    sems[k] = es.enter_context(nc.semaphore("d_" + o.key))
                cnt[k] = cnt.get(k, 0) + 16
                o.tok = (k, cnt[k])
            else:
                k = "c:" + o.key
                sems[k] = es.enter_context(nc.semaphore("c_" + o.key))
                o.tok = (k, 1)
    final_toks = [o.tok for o in B.sinks()]

    def emit(name, eng, final=False):
        waited = {}

        def wait(tok):
            sname, val = tok
            if waited.get(sname, 0) >= val:
                return
            waited[sname] = val
            eng.wait_ge(sems[sname], val)

        for o in order[name]:
            toks = set()
            for d in o.raw:
                if d.kind == "eng" and d.stream == name and name == "pe":
                    continue
                toks.add(d.tok)
            for d in o.war:
                if d.kind == "eng" and d.stream == name and (name == "pe" or not SAMEWAR):
                    continue
                toks.add(d.tok)
            for t in sorted(toks):
                wait(t)
            ins = o.fn(eng)
            ins.then_inc(sems[o.tok[0]], 16 if o.kind == "dma" else 1)
        if final:
            for t in sorted(final_toks):
                wait(t)

    with nc.Block() as block:
        @block.sync
        def _(e):
            emit("sp", e, final=True)

        @block.tensor
        def _(e):
            emit("pe", e)

        @block.scalar
        def _(e):
            emit("act", e)

        @block.vector
        def _(e):
            emit("dve", e)

        @block.gpsimd
        def _(e):
            emit("pool", e)

    es.close()
    nc._B = B
    nc._order = order
    return nc


_PERM = np.concatenate([np.arange(0, 128, 2), np.arange(1, 128, 2)])
_OFF = dict(qA=0, kA=1024, vA=2048, zA=4096, lr=6144, qB=6160, kB=7184, vB=8208, zB=10256, ga=12304, gb=14352)


def _unit_cols(hf):
    units = []
    for h in range(2):
        H = 2 * hf + h
        ar = np.concatenate([_OFF["qA"] + H * 256 + np.arange(256), _OFF["kA"] + H * 256 + np.arange(256),
                             _OFF["vA"] + H * 512 + np.arange(512), _OFF["lr"] + np.arange(16)])
        ag = np.concatenate([_OFF["zA"] + H * 512 + np.arange(512), _OFF["ga"] + H * 512 + np.arange(512)])
        units += [ar, ag]
        for jj in range(2):
            J = 4 * hf + 2 * h + jj
            rt = np.concatenate([_OFF["qB"] + J * 128 + _PERM, _OFF["kB"] + J * 128 + _PERM,
                                 _OFF["vB"] + J * 256 + np.arange(256), _OFF["zB"] + J * 256 + np.arange(256),
                                 _OFF["gb"] + J * 256 + np.arange(256)])
            units.append(rt)
    return units


def _consts(hf):
    c = np.zeros((128, 1024), np.float32)
    s = np.arange(128)[:, None]
    t = np.arange(128)[None, :]
    c[:, 0:128] = np.where(s <= t, -1.0 / 16, 0.0)
    same = (s // 4) == (t // 4)
    c[:, 128:256] = np.where(same & (s <= t), -1.0 / 16, 0.0)
    q = np.arange(32)[None, :]
    c[:, 256:288] = ((s // 4) == q).astype(np.float32)
    c[:, 288:320] = c[:, 256:288] * (-1.0 / 16)
    c[:, 320:322] = -1.0 / 16
    c[:, 384:512] = 1.0
    tt_ = np.arange(128, dtype=np.float64)
    for j in range(4):
        J = 4 * hf + j
        g = 1.0 - 2.0 ** (-5.0 - J)
        lg = math.log(g)
        c[:, 512 + 8 * j + 0] = np.exp(lg * (tt_ + 1))
        c[:, 512 + 8 * j + 1] = np.exp(-lg * (tt_ + 1)) * 128.0 ** -0.5
        c[:, 512 + 8 * j + 2] = np.exp(lg * (tt_ % 4 + 1))
        c[:, 512 + 8 * j + 3] = np.exp(-lg * (tt_ % 4 + 1)) * 128.0 ** -0.5
        c[:, 512 + 8 * j + 4] = g ** 128
        c[:, 512 + 8 * j + 5] = g ** 4
    c[:, 544] = math.log(1.0 / 16)
    c[:, 545] = HN_EPS
    c[:, 546] = LN_EPS
    c[:, 547] = 1.0
    c[:, 548] = 1.0 - hf
    c[:, 549] = float(hf)
    import ml_dtypes
    cb = np.zeros((128, 512), np.float32)
    cb[:, 0:128] = np.eye(128)
    cb[:, 128:256] = (s <= t)
    cb[:, 256:384] = same & (s <= t)
    return c, cb.astype(ml_dtypes.bfloat16)


def _rope_tables(hf):
    inv = (1.0 / (10000.0 ** np.linspace(0.0, 1.0, 64, dtype=np.float32))).astype(np.float32)
    out = np.zeros((4, NT, 128, 512), np.float32)
    for i in range(NT):
        if i < 16:
            pos = (i * 128 + np.arange(128)).astype(np.float32)
            tau = np.arange(128, dtype=np.float64)
        else:
            pos = (16384 + np.arange(128) % 4).astype(np.float32)
            tau = (np.arange(128) % 4).astype(np.float64)
        ang = (pos[:, None] * inv[None, :]).astype(np.float32)
        cs, sn = np.cos(ang.astype(np.float64)), np.sin(ang.astype(np.float64))
        cos2 = np.concatenate([cs, cs], axis=1)
        sin2 = np.concatenate([-sn, sn], axis=1)
        for j in range(4):
            J = 4 * hf + j
            lg = math.log(1.0 - 2.0 ** (-5.0 - J))
            ebq = np.exp(lg * (tau + 1))[:, None]
            ebk = (np.exp(-lg * (tau + 1)) * 128.0 ** -0.5)[:, None]
            out[j, i, :, 0:128] = ebq * cos2
            out[j, i, :, 128:256] = ebk * cos2
            out[j, i, :, 256:384] = ebq * sin2
            out[j, i, :, 384:512] = ebk * sin2
    return out


_NC_CACHE = {}


def kernel(x_prompt, x_sample, state_gla, state_ret, w_in, w_lr, b_lr, gla_norm_w,
           ret_norm_w, ret_norm_b, w_out, ln_w, ln_b):
    f32 = np.float32
    x_prompt = np.asarray(x_prompt, f32)
    x_sample = np.asarray(x_sample, f32)
    state_gla = np.asarray(state_gla, f32)
    state_ret = np.asarray(state_ret, f32)
    w_in = np.asarray(w_in, f32)[0]
    w_lr = np.asarray(w_lr, f32)[0]
    b_lr = np.asarray(b_lr, f32)[0]
    gla_norm_w = np.asarray(gla_norm_w, f32)
    ret_norm_w = np.asarray(ret_norm_w, f32)[0]
    ret_norm_b = np.asarray(ret_norm_b, f32)[0]
    w_out = np.asarray(w_out, f32)[0]
    ln_w = np.asarray(ln_w, f32)
    ln_b = np.asarray(ln_b, f32)

    if "nc" not in _NC_CACHE:
        _NC_CACHE["nc"] = build_program()
    nc = _NC_CACHE["nc"]
    ropes = [_rope_tables(0), _rope_tables(1)]
    w_out_p = np.ascontiguousarray(w_out.reshape(2, 4, 256, D).transpose(1, 0, 2, 3).reshape(D, D))

    in_maps = []
    for c in range(8):
        b, hf = c // 2, c % 2
        xs = x_sample[32 * b:32 * b + 32].reshape(128, D)
        X = np.concatenate([x_prompt[b], xs], axis=0)
        xT = X.reshape(NT, 128, 16, 128).transpose(0, 3, 2, 1)
        xT = np.ascontiguousarray(xT).reshape(NT, 128, 2048)
        xres = np.concatenate([x_prompt[b, hf * 1024:(hf + 1) * 1024], xs[hf * 64:(hf + 1) * 64]], axis=0)
        wu = np.zeros((NU, D, WC), f32)
        for u, cols in enumerate(_unit_cols(hf)):
            wu[u, :, :len(cols)] = w_in[:, cols]
        wu[8, :, :1024] = w_out_p[:, 0:1024]
        wu[9, :, :1024] = w_out_p[:, 1024:2048]
        cst, cstb = _consts(hf)
        H0, J0 = 2 * hf, 4 * hf
        in_maps.append({
            "xT": xT, "xres": np.ascontiguousarray(xres), "wu": wu,
            "wlr": np.ascontiguousarray(w_lr[:, H0 * 256:H0 * 256 + 512]),
            "blr": np.ascontiguousarray(b_lr[None, H0 * 256:H0 * 256 + 512]),
            "gw": np.ascontiguousarray(gla_norm_w.reshape(1, 512)),
            "rw": np.ascontiguousarray(ret_norm_w[None, J0 * 256:J0 * 256 + 1024]),
            "rb": np.ascontiguousarray(ret_norm_b[None, J0 * 256:J0 * 256 + 1024]),
            "lnw": np.ascontiguousarray(ln_w.reshape(1, D)), "lnb": np.ascontiguousarray(ln_b.reshape(1, D)),
            "sgla": np.ascontiguousarray(state_gla[0, 32 * b:32 * b + 32, H0:H0 + 2]),
            "sret": np.ascontiguousarray(state_ret[0, 32 * b:32 * b + 32, J0:J0 + 4][:, :, _PERM, :]),
            "rope": ropes[hf], "cst": cst, "cstb": cstb,
        })

    res = run_bass_kernel_spmd(nc, in_maps, core_ids=list(range(8)))
    R = res.results

    y_p = np.zeros((4, 2048, D), f32)
    y_s = np.zeros((128, 4, D), f32)
    g_p = np.zeros((1, 4, 4, 256, 512), f32)
    r_p = np.zeros((1, 4, 8, 128, 256), f32)
    g_s = np.zeros((1, 128, 4, 256, 512), f32)
    r_s = np.zeros((1, 128, 8, 128, 256), f32)
    inv = np.argsort(_PERM)
    for c in range(8):
        b, hf = c // 2, c % 2
        H0, J0 = 2 * hf, 4 * hf
        y = np.asarray(R[c]["y"], f32)
        y_p[b, hf * 1024:(hf + 1) * 1024] = y[0:1024]
        y_s[32 * b + 16 * hf:32 * b + 16 * hf + 16] = y[1024:1088].reshape(16, 4, D)
        g_p[0, b, H0:H0 + 2] = np.asarray(R[c]["sgp"], f32)
        r_p[0, b, J0:J0 + 4] = np.asarray(R[c]["srp"], f32)[:, inv, :]
        g_s[0, 32 * b:32 * b + 32, H0:H0 + 2] = np.asarray(R[c]["sgs"], f32)
        r_s[0, 32 * b:32 * b + 32, J0:J0 + 4] = np.asarray(R[c]["srs"], f32)[:, :, inv, :]
    return (y_p, y_s, g_p, r_p, g_s, r_s)
```
